# Optimizing a Trainium2 kernel written in Bass

```python
import jax
import jax.numpy as jnp
from jax import lax
import numpy as np

D_MODEL = 1024
BATCH = 8
SEQ = 2048
DEPTH = 4

N_MIXERS = 3
N_MLA_LAYERS = (DEPTH + 2) // 3
N_SWA_LAYERS = (DEPTH + 1) // 3
N_FOX_LAYERS = DEPTH // 3
BLOCK = 128
NEG_INF = -1e30
RMS_EPS = 1e-6
LN_EPS = 1e-5
DEEPNORM_ALPHA = (2 * DEPTH) ** 0.25
DEEPNORM_BETA = (8 * DEPTH) ** -0.25

MLA_HEADS = 16
MLA_NOPE_DIM = 64
MLA_ROPE_DIM = 32
MLA_V_DIM = 64
MLA_Q_RANK = 384
MLA_KV_RANK = 256
ROPE_THETA = 10000.0
MLA_WIDTH = MLA_HEADS * MLA_V_DIM
MLA_IN = MLA_Q_RANK + MLA_KV_RANK + MLA_ROPE_DIM + MLA_WIDTH

SWA_Q_HEADS = 16
SWA_KV_HEADS = 4
SWA_GROUP = SWA_Q_HEADS // SWA_KV_HEADS
SWA_HEAD_DIM = 64
WINDOW = 128
SWA_WIDTH = SWA_Q_HEADS * SWA_HEAD_DIM
SWA_KV_WIDTH = SWA_KV_HEADS * SWA_HEAD_DIM
SWA_IN = 2 * SWA_WIDTH + 2 * SWA_KV_WIDTH

FOX_HEADS = 16
FOX_HEAD_DIM = 64
FOX_WIDTH = FOX_HEADS * FOX_HEAD_DIM
FOX_IN = 4 * FOX_WIDTH + FOX_HEADS

kernel_name = "hybrid_mla_swa_fox_deepnorm"


def _rmsnorm(x, g):
    xf = x.astype(jnp.float32)
    y = xf * lax.rsqrt(jnp.mean(xf * xf, axis=-1, keepdims=True) + RMS_EPS)
    return (y * g.astype(jnp.float32)).astype(x.dtype)


def _layernorm(x, g, b):
    xf = x.astype(jnp.float32)
    mu = jnp.mean(xf, axis=-1, keepdims=True)
    var = jnp.mean(jnp.square(xf - mu), axis=-1, keepdims=True)
    y = (xf - mu) * lax.rsqrt(var + LN_EPS) * g.astype(jnp.float32) + b.astype(jnp.float32)
    return y.astype(x.dtype)


def _rope_tables(seq, dim):
    inv = ROPE_THETA ** (-jnp.arange(0, dim, 2, dtype=jnp.float32) / dim)
    ang = jnp.arange(seq, dtype=jnp.float32)[:, None] * inv[None, :]
    return jnp.cos(ang), jnp.sin(ang)


def _apply_rope(x, cos, sin):
    x1, x2 = jnp.split(x.astype(jnp.float32), 2, axis=-1)
    return jnp.concatenate([x1 * cos - x2 * sin, x2 * cos + x1 * sin], axis=-1).astype(x.dtype)


def _alibi_slopes(n_heads):
    return 2.0 ** (-8.0 * jnp.arange(1, n_heads + 1, dtype=jnp.float32) / n_heads)


def _dense_causal(q, k, v, scale, cum=None):
    B, S, H, _ = q.shape
    dv = v.shape[-1]
    nb = S // BLOCK
    k_pos = jnp.arange(S)
    cum_t = None if cum is None else jnp.transpose(cum, (0, 2, 1))

    def one_block(n):
        start = n * BLOCK
        q_blk = lax.dynamic_slice_in_dim(q, start, BLOCK, axis=1)
        q_pos = start + jnp.arange(BLOCK)
        s = jnp.einsum('bqhd,bkhd->bhqk', q_blk, k, preferred_element_type=jnp.float32) * scale
        if cum_t is not None:
            c_q = lax.dynamic_slice_in_dim(cum_t, start, BLOCK, axis=2)
            s = s + c_q[..., :, None] - cum_t[:, :, None, :]
        s = jnp.where(k_pos[None, :] <= q_pos[:, None], s, NEG_INF)
        p = jax.nn.softmax(s, axis=-1).astype(v.dtype)
        return jnp.einsum('bhqk,bkhd->bqhd', p, v)

    out = lax.map(one_block, jnp.arange(nb))
    return jnp.transpose(out, (1, 0, 2, 3, 4)).reshape(B, S, H, dv)


def _mla_branch(x, w_in, q_norm, w_q_up, kv_norm, w_kv_up, w_out):
    B, S, _ = x.shape
    h = x @ w_in
    c1 = MLA_Q_RANK
    c2 = c1 + MLA_KV_RANK
    c3 = c2 + MLA_ROPE_DIM
    cq, ckv, k_rope, gate = h[..., :c1], h[..., c1:c2], h[..., c2:c3], h[..., c3:]
    q = (_rmsnorm(cq, q_norm) @ w_q_up).reshape(B, S, MLA_HEADS, MLA_NOPE_DIM + MLA_ROPE_DIM)
    kv = (_rmsnorm(ckv, kv_norm) @ w_kv_up).reshape(B, S, MLA_HEADS, MLA_NOPE_DIM + MLA_V_DIM)
    cos, sin = _rope_tables(S, MLA_ROPE_DIM)
    q_rope = _apply_rope(q[..., MLA_NOPE_DIM:], cos[:, None, :], sin[:, None, :])
    k_rope = _apply_rope(k_rope, cos, sin)
    q_full = jnp.concatenate([q[..., :MLA_NOPE_DIM], q_rope], axis=-1)
    k_full = jnp.concatenate(
        [kv[..., :MLA_NOPE_DIM],
         jnp.broadcast_to(k_rope[:, :, None, :], (B, S, MLA_HEADS, MLA_ROPE_DIM))], axis=-1)
    v = kv[..., MLA_NOPE_DIM:]
    scale = (MLA_NOPE_DIM + MLA_ROPE_DIM) ** -0.5
    o = _dense_causal(q_full, k_full, v, scale).reshape(B, S, MLA_WIDTH)
    return (o * jax.nn.silu(gate)) @ w_out


def _swa_branch(x, w_in, sinks, w_out):
    B, S, _ = x.shape
    nb = S // BLOCK
    h = x @ w_in
    c1 = SWA_WIDTH
    c2 = c1 + SWA_KV_WIDTH
    c3 = c2 + SWA_KV_WIDTH
    q = h[..., :c1].reshape(B, nb, BLOCK, SWA_KV_HEADS, SWA_GROUP, SWA_HEAD_DIM)
    k = h[..., c1:c2].reshape(B, nb, BLOCK, SWA_KV_HEADS, SWA_HEAD_DIM)
    v = h[..., c2:c3].reshape(B, nb, BLOCK, SWA_KV_HEADS, SWA_HEAD_DIM)
    gate = h[..., c3:]

    def band(t):
        prev = jnp.concatenate([jnp.zeros_like(t[:, :1]), t[:, :-1]], axis=1)
        return jnp.concatenate([prev, t], axis=2)

    kb, vb = band(k), band(v)
    scale = SWA_HEAD_DIM ** -0.5
    s = jnp.einsum('bnqkgd,bnskd->bnkgqs', q, kb, preferred_element_type=jnp.float32) * scale
    qi = jnp.arange(BLOCK)[:, None]
    kj = jnp.arange(2 * BLOCK)[None, :]
    dist = qi + BLOCK - kj
    key_pos = jnp.arange(nb)[:, None, None] * BLOCK - BLOCK + kj[None]
    valid = (dist >= 0) & (dist < WINDOW) & (key_pos >= 0)
    slopes = _alibi_slopes(SWA_Q_HEADS).reshape(SWA_KV_HEADS, SWA_GROUP)
    s = s - slopes[:, :, None, None] * dist.astype(jnp.float32)
    s = jnp.where(valid[None, :, None, None], s, NEG_INF)
    sink = sinks.astype(jnp.float32).reshape(SWA_KV_HEADS, SWA_GROUP)[:, :, None, None]
    m = jnp.maximum(jnp.max(s, axis=-1, keepdims=True), sink)
    p = jnp.exp(s - m)
    p = (p / (jnp.sum(p, axis=-1, keepdims=True) + jnp.exp(sink - m))).astype(vb.dtype)
    o = jnp.einsum('bnkgqs,bnskd->bnqkgd', p, vb).reshape(B, S, SWA_WIDTH)
    return (o * jax.nn.silu(gate)) @ w_out


def _fox_branch(x, w_in, b_f, w_out):
    B, S, _ = x.shape
    h = x @ w_in
    W = FOX_WIDTH
    q = h[..., :W].reshape(B, S, FOX_HEADS, FOX_HEAD_DIM)
    k = h[..., W:2 * W].reshape(B, S, FOX_HEADS, FOX_HEAD_DIM)
    v = h[..., 2 * W:3 * W].reshape(B, S, FOX_HEADS, FOX_HEAD_DIM)
    f_logit = h[..., 3 * W:3 * W + FOX_HEADS]
    gate = h[..., 3 * W + FOX_HEADS:]
    log_f = jax.nn.log_sigmoid(f_logit.astype(jnp.float32) + b_f.astype(jnp.float32))
    cum = jnp.cumsum(log_f, axis=1)
    o = _dense_causal(q, k, v, FOX_HEAD_DIM ** -0.5, cum).reshape(B, S, W)
    return (o * jax.nn.silu(gate)) @ w_out


def setup_inputs(seed: int = 0) -> dict:
    key = jax.random.key(seed)
    ks = jax.random.split(key, 16)

    def nrm(k, shape, scale):
        return jax.random.normal(k, shape, jnp.float32) * scale

    return {
        "x": nrm(ks[0], (BATCH, SEQ, D_MODEL), 1.0),
        "ln_g": 1.0 + nrm(ks[1], (DEPTH, D_MODEL), 0.02),
        "ln_b": nrm(ks[2], (DEPTH, D_MODEL), 0.02),
        "mla_w_in": nrm(ks[3], (N_MLA_LAYERS, D_MODEL, MLA_IN), D_MODEL ** -0.5),
        "mla_q_norm": 1.0 + nrm(ks[4], (N_MLA_LAYERS, MLA_Q_RANK), 0.02),
        "mla_w_q_up": nrm(ks[5], (N_MLA_LAYERS, MLA_Q_RANK, MLA_HEADS * (MLA_NOPE_DIM + MLA_ROPE_DIM)), MLA_Q_RANK ** -0.5),
        "mla_kv_norm": 1.0 + nrm(ks[6], (N_MLA_LAYERS, MLA_KV_RANK), 0.02),
        "mla_w_kv_up": nrm(ks[7], (N_MLA_LAYERS, MLA_KV_RANK, MLA_HEADS * (MLA_NOPE_DIM + MLA_V_DIM)), MLA_KV_RANK ** -0.5),
        "mla_w_out": nrm(ks[8], (N_MLA_LAYERS, MLA_WIDTH, D_MODEL), DEEPNORM_BETA * MLA_WIDTH ** -0.5),
        "swa_w_in": nrm(ks[9], (N_SWA_LAYERS, D_MODEL, SWA_IN), D_MODEL ** -0.5),
        "swa_sinks": nrm(ks[10], (N_SWA_LAYERS, SWA_Q_HEADS), 0.5),
        "swa_w_out": nrm(ks[11], (N_SWA_LAYERS, SWA_WIDTH, D_MODEL), DEEPNORM_BETA * SWA_WIDTH ** -0.5),
        "fox_w_in": nrm(ks[12], (N_FOX_LAYERS, D_MODEL, FOX_IN), D_MODEL ** -0.5),
        "fox_b_f": jax.random.uniform(ks[13], (N_FOX_LAYERS, FOX_HEADS), jnp.float32, 1.0, 4.0),
        "fox_w_out": nrm(ks[14], (N_FOX_LAYERS, FOX_WIDTH, D_MODEL), DEEPNORM_BETA * FOX_WIDTH ** -0.5),
    }


def reference(x, ln_g, ln_b, mla_w_in, mla_q_norm, mla_w_q_up, mla_kv_norm, mla_w_kv_up, mla_w_out,
              swa_w_in, swa_sinks, swa_w_out, fox_w_in, fox_b_f, fox_w_out):
    for i in range(DEPTH):
        j = i // N_MIXERS
        kind = i % N_MIXERS
        if kind == 0:
            y = _mla_branch(x, mla_w_in[j], mla_q_norm[j], mla_w_q_up[j], mla_kv_norm[j],
                            mla_w_kv_up[j], mla_w_out[j])
        elif kind == 1:
            y = _swa_branch(x, swa_w_in[j], swa_sinks[j], swa_w_out[j])
        else:
            y = _fox_branch(x, fox_w_in[j], fox_b_f[j], fox_w_out[j])
        x = _layernorm(DEEPNORM_ALPHA * x + y, ln_g[i], ln_b[i])
    return x
```

```python
import os
import numpy as np
import concourse.bass as bass
import concourse.mybir as mybir
from concourse.bass_utils import run_bass_kernel_spmd

F32 = mybir.dt.float32
BF16 = mybir.dt.bfloat16
AF = mybir.ActivationFunctionType
ALU = mybir.AluOpType

S = 2048
D = 1024
NT = S // 128
NGRP = S // 512
DEPTH = 4
ALPHA = float((2 * DEPTH) ** 0.25)
LN_EPS = 1e-5
RMS_EPS = 1e-6
NEG = -30000.0


class Trk:
    ENG = ("pe", "act", "dve", "pool", "sp")
    CMP = ("pe", "act", "dve", "pool")

    def __init__(self, nc):
        self.nc = nc
        self.streams = {e: [] for e in self.ENG}
        self.cnt = {e: 0 for e in self.ENG}
        self.esem = {}
        self.dsem = {}
        self.dcnt = {}
        self.last_w = {}
        self.readers = {}
        self.seen = {e: {} for e in self.ENG}

    def sem(self, key):
        if key in self.ENG:
            if key not in self.esem:
                self.esem[key] = self.nc.alloc_semaphore("cs_" + key)
            return self.esem[key]
        if key not in self.dsem:
            self.dsem[key] = self.nc.alloc_semaphore("ds_" + key)
            self.dcnt[key] = 0
        return self.dsem[key]

    def _need(self, e, ev, waits):
        if ev is None:
            return
        k, v = ev
        if k == e and e == "pe":
            return
        if self.seen[e].get(k, 0) >= v:
            return
        if waits.get(k, 0) < v:
            waits[k] = v

    def _deps(self, e, reads, writes):
        waits = {}
        for r in reads:
            self._need(e, self.last_w.get(r), waits)
        for w in writes:
            self._need(e, self.last_w.get(w), waits)
            for k, v in self.readers.get(w, {}).items():
                self._need(e, (k, v), waits)
        for k, v in waits.items():
            self.seen[e][k] = v
            self.streams[e].append(("wait", k, v))

    def _commit(self, ev, reads, writes):
        for r in reads:
            d = self.readers.setdefault(r, {})
            if d.get(ev[0], 0) < ev[1]:
                d[ev[0]] = ev[1]
        for w in writes:
            self.last_w[w] = ev
            self.readers[w] = {}

    def op(self, e, fn, reads=(), writes=()):
        self._deps(e, reads, writes)
        self.sem(e)
        self.cnt[e] += 1
        ev = (e, self.cnt[e])
        self.streams[e].append(("op", fn, e))
        self._commit(ev, reads, writes)
        return ev

    def dma(self, q, slot, fn, reads=(), writes=(), n=1):
        self._deps(q, reads, writes)
        self.sem(slot)
        self.dcnt[slot] += 16 * n
        ev = (slot, self.dcnt[slot])
        self.streams[q].append(("dma", fn, slot))
        self._commit(ev, reads, writes)
        return ev

    def barrier(self):
        for e in self.ENG:
            for k in self.CMP:
                if k != e and k in self.esem and self.cnt[k] > self.seen[e].get(k, 0):
                    self.seen[e][k] = self.cnt[k]
                    self.streams[e].append(("wait", k, self.cnt[k]))

    def wait_all(self, e):
        for k in list(self.esem):
            if k != e and self.cnt[k] > self.seen[e].get(k, 0):
                self.seen[e][k] = self.cnt[k]
                self.streams[e].append(("wait", k, self.cnt[k]))
        for k in self.dsem:
            if self.dcnt[k] > self.seen[e].get(k, 0):
                self.seen[e][k] = self.dcnt[k]
                self.streams[e].append(("wait", k, self.dcnt[k]))

    def emit(self, block):
        tr = self

        def run(eng, e):
            for it in tr.streams[e]:
                if it[0] == "wait":
                    eng.wait_ge(tr.sem(it[1]), it[2])
                elif it[0] == "op":
                    ins = it[1](eng)
                    ins.then_inc(tr.esem[e], 1)
                else:
                    inss = it[1](eng)
                    if not isinstance(inss, (list, tuple)):
                        inss = [inss]
                    for i in inss:
                        i.then_inc(tr.dsem[it[2]], 16)

        @block.tensor
        def _(eng):
            run(eng, "pe")

        @block.scalar
        def _(eng):
            run(eng, "act")

        @block.vector
        def _(eng):
            run(eng, "dve")

        @block.gpsimd
        def _(eng):
            run(eng, "pool")

        @block.sync
        def _(eng):
            run(eng, "sp")


class Ring:
    def __init__(self, items):
        self.items = items
        self.i = 0

    def next(self):
        it = self.items[self.i % len(self.items)]
        self.i += 1
        return it


def _chunk(w):
    k, n = w.shape
    return np.ascontiguousarray(w.reshape(k // 128, 128, n).transpose(1, 0, 2).reshape(128, -1))


KINDS = [0, 1, 2, 0]
JIDX = [0, 0, 0, 1]

M_WL = 0
M_WQ = M_WL + 8 * 672
M_WQS = M_WQ + 3 * 1536
M_WKK = M_WQS + 3 * 576
M_WKV = M_WKK + 2 * 1024
M_END = M_WKV + 2 * 1024
F_WQ = 0
F_WK = F_WQ + 8 * 256
F_WV = F_WK + 8 * 256
F_END = F_WV + 8 * 256
W_WQ = 0
W_WK = W_WQ + 8 * 256
W_WV = W_WK + 8 * 64
W_END = W_WV + 8 * 64
SLOT = 6400


def _host_consts():
    c = {}
    c["ident"] = np.eye(128, dtype=np.float32)
    k = np.arange(128)[:, None]
    q = np.arange(128)[None, :]
    c["cmask"] = np.where(k <= q, 0.0, NEG).astype(np.float32)
    c["u32"] = (k <= q).astype(np.float32)
    c["ones32"] = np.ones((128, 128), np.float32)
    prev = np.where(k > q, -8.0 * (q + 128 - k), -1e9).astype(np.float32)
    cur = np.where(k <= q, -8.0 * (q - k), -1e9).astype(np.float32)
    dm = np.concatenate([prev, cur, prev, cur], axis=1)
    dmf = np.concatenate([np.full((128, 128), -1e9, np.float32), cur, prev, cur], axis=1)
    c["dm4"] = np.ascontiguousarray(dm)
    c["dm4f"] = np.ascontiguousarray(dmf)
    slopes = (2.0 ** (-8.0 * np.arange(1, 17, dtype=np.float32) / 16)).astype(np.float32)
    c["slopes"] = slopes
    inv = (10000.0 ** (-np.arange(0, 32, 2, dtype=np.float32) / 32)).astype(np.float32)
    ang = np.arange(S, dtype=np.float32)[:, None] * inv[None, :]
    cos = np.cos(ang).astype(np.float32)
    sin = np.sin(ang).astype(np.float32)
    cc = np.concatenate([cos, cos], axis=1)
    ss = np.concatenate([-sin, sin], axis=1)
    c["cc"] = np.ascontiguousarray(cc.reshape(NT, 128, 32).transpose(1, 0, 2).reshape(128, -1))
    c["ss"] = np.ascontiguousarray(ss.reshape(NT, 128, 32).transpose(1, 0, 2).reshape(128, -1))
    c["cosf"] = np.ascontiguousarray(cc.T)
    c["sinf"] = np.ascontiguousarray(ss.T)
    return c


def _host_weights(inp):
    w = {}
    for L in range(DEPTH):
        kind, j = KINDS[L], JIDX[L]
        if kind == 0:
            w_in = inp["mla_w_in"][j]
            wl = np.concatenate([w_in[:, 0:384], w_in[:, 640:672], w_in[:, 384:640]], axis=1)
            wq = inp["mla_w_q_up"][j]
            wqs = np.zeros((384, 576), np.float32)
            for h in range(16):
                base = h * 96 + 64
                wqs[:, 64 + h * 32: 64 + h * 32 + 16] = wq[:, base + 16: base + 32]
                wqs[:, 64 + h * 32 + 16: 64 + h * 32 + 32] = wq[:, base: base + 16]
            wkv = inp["mla_w_kv_up"][j].reshape(256, 16, 128)
            wkk = wkv[:, :, 0:64].reshape(256, 1024)
            wkvv = wkv[:, :, 64:128].reshape(256, 1024)
            arena = np.concatenate([_chunk(wl), _chunk(wq), _chunk(wqs), _chunk(wkk), _chunk(wkvv)], axis=1)
            assert arena.shape[1] == M_END
            w[f"wb{L}"] = np.ascontiguousarray(arena)
            w[f"wg{L}"] = _chunk(w_in[:, 672:1696])
            w[f"wo{L}"] = _chunk(inp["mla_w_out"][j])
            w[f"gq{L}"] = np.concatenate([inp["mla_q_norm"][j], inp["mla_kv_norm"][j]]).astype(np.float32)
        elif kind == 1:
            w_in = inp["swa_w_in"][j]
            slots = []
            for g in range(4):
                wq = w_in[:, g * 256:(g + 1) * 256]
                wk = w_in[:, 1024 + g * 64: 1024 + (g + 1) * 64]
                wv = w_in[:, 1280 + g * 64: 1280 + (g + 1) * 64]
                slots.append(np.concatenate([_chunk(wq), _chunk(wk), _chunk(wv)], axis=1))
            w[f"wb{L}"] = np.ascontiguousarray(np.stack(slots, 0))
            w[f"wg{L}"] = _chunk(w_in[:, 1536:2560])
            w[f"wo{L}"] = _chunk(inp["swa_w_out"][j])
            w[f"sinks{L}"] = inp["swa_sinks"][j].astype(np.float32)
        else:
            w_in = inp["fox_w_in"][j]
            slots = []
            for g in range(4):
                wq = w_in[:, g * 256:(g + 1) * 256]
                wk = w_in[:, 1024 + g * 256: 1024 + (g + 1) * 256]
                wv = w_in[:, 2048 + g * 256: 2048 + (g + 1) * 256]
                slots.append(np.concatenate([_chunk(wq), _chunk(wk), _chunk(wv)], axis=1))
            w[f"wb{L}"] = np.ascontiguousarray(np.stack(slots, 0))
            w[f"wf{L}"] = _chunk(w_in[:, 3072:3088])
            w[f"wg{L}"] = _chunk(w_in[:, 3088:4112])
            w[f"wo{L}"] = _chunk(inp["fox_w_out"][j])
            w[f"bf{L}"] = inp["fox_b_f"][j].astype(np.float32)
    w["lng"] = np.ascontiguousarray(inp["ln_g"].astype(np.float32))
    w["lnb"] = np.ascontiguousarray(inp["ln_b"].astype(np.float32))
    return w


def build_program(nlayers=DEPTH, shapes=None):
    nc = bass.Bass("TRN2", target_bir_lowering=False)
    t = Trk(nc)
    dr = {}

    def din(name, shape):
        dr[name] = nc.dram_tensor(name, list(shape), F32, kind="ExternalInput").ap()
        return dr[name]

    x_in = din("x", [S, D])
    for name, shp in shapes.items():
        din(name, shp)
    y_out = nc.dram_tensor("y", [S, D], F32, kind="ExternalOutput").ap()
    scratch = [nc.dram_tensor(f"xs{i}", [S, D], F32).ap() for i in range(2)]

    def sb(name, shape, dt):
        return nc.alloc_sbuf_tensor("sb_" + name, list(shape), dt)

    R32 = sb("R32", [128, 16384], BF16)
    Q32 = sb("Q32", [128, 16384], BF16)
    GB = sb("GB", [128, NT, 1024], BF16)
    VA = sb("VA", [128, NT, 4, 65], BF16)
    X16 = sb("X16", [128, 4096], F32)
    WX = sb("WX", [128, 8192], BF16)
    WB = sb("WB", [128, M_END], BF16)
    xs_t = [sb(f"xs_sb{i}", [128, 1024], F32) for i in range(2)]
    pt_t = [sb(f"pt{i}", [128, 512], BF16) for i in range(3)]
    ln_t = [sb(f"lnt{i}", [128, 768], BF16) for i in range(2)]
    og_t = [sb(f"ogt{i}", [128, 1024], BF16) for i in range(3)]
    ident = sb("ident", [128, 128], BF16)
    cmask = sb("cmask", [128, 128], BF16)
    u32 = sb("u32", [128, 128], F32)
    ones32 = sb("ones32", [128, 128], F32)
    dm4 = X16[:, 0:512]
    dm4f = X16[:, 512:1024]
    swring = Ring([(X16[:, 1024 + i * 512:1536 + i * 512], f"swt{i}") for i in range(3)])
    gq = sb("gq", [128, 640], F32)
    cc = sb("cc", [128, NT, 32], F32)
    ssg = sb("ssg", [128, NT, 32], F32)
    small = sb("small", [128, 256], F32)
    tmp_t = [sb(f"tmpn{i}", [128, 4, 64], F32) for i in range(2)]
    junk = sb("junk", [128, 384], BF16)
    rk = sb("rk", [128, 128], F32)
    qtm_t = [sb(f"qtm{i}", [128, 260], BF16) for i in range(2)]
    ktm_t = [sb(f"ktm{i}", [128, 256], BF16) for i in range(2)]
    qtring = Ring([(qtm_t[i], f"qtm{i}") for i in range(2)])
    ktring = Ring([(ktm_t[i], f"ktm{i}") for i in range(2)])

    ms = small[:, 0:2]
    lnms = small[:, 2:4]
    rstd2 = small[:, 4:6]
    lnr = small[:, 6:8]
    rc_t = [small[:, 8:12], small[:, 12:16]]
    slopes_bc = small[:, 16:32]
    sinks_bc = small[:, 32:48]
    esink = small[:, 48:64]
    bf_bc = small[:, 64:80]
    zt = small[:, 80:96]
    et = small[:, 96:112]
    carry = [small[:, 112:128], small[:, 128:144]]
    stats = small[:, 144:156]
    mv = small[:, 156:158]
    den_t = small[:, 160:164]


    PS = [nc.alloc_psum_tensor(f"ps{i}", [128, 512], F32) for i in range(6)]
    PT_ = [nc.alloc_psum_tensor(f"pst{i}", [128, 1024], BF16) for i in range(2)]
    short = Ring([(PS[i], f"ps{i}") for i in range(4)])
    longr = Ring([(PS[4], "ps4"), (PS[5], "ps5")])
    proj6 = Ring([(PS[i], f"ps{i}") for i in range(6)])
    tring = Ring([(PT_[0], "pst0"), (PT_[1], "pst1")])
    ptring = Ring([(pt_t[i], f"pt{i}") for i in range(3)])
    xsring = Ring([(xs_t[i], f"xs_sb{i}") for i in range(2)])
    lnring = Ring([(ln_t[i], f"lnt{i}") for i in range(2)])
    ogring = Ring([(og_t[i], f"ogt{i}") for i in range(3)])
    rcring = Ring([(rc_t[i], f"rc{i}") for i in range(2)])
    tmpring = Ring([(tmp_t[i], f"tmpn{i}") for i in range(2)])
    evtog = [0]

    def evac_engine():
        evtog[0] += 1
        return "dve" if evtog[0] % 3 else "act"

    def copy_op(e, out, in_, reads, writes):
        if e == "act":
            t.op("act", lambda eng: eng.activation(out=out, in_=in_, func=AF.Copy), reads, writes)
        else:
            t.op(e, lambda eng: eng.tensor_copy(out=out, in_=in_), reads, writes)

    xT = R32[:, :].rearrange("p (c n) -> p c n", c=8)
    xT_res = [f"xT{i}" for i in range(NT)]
    X16_ALL = ["X16", "Zg", "xo", "lng", "lnb", "z0", "z1", "cum8", "Wf", "swt0", "swt1", "swt2", "dm4", "dm4f", "u32x", "carry0"]

    def xT_reads(G):
        return xT_res[4 * G:4 * G + 4]

    def load_consts():
        t.dma("pool", "c_id", lambda e: [e.dma_start(out=ident[:, :], in_=dr["ident"][:, :]),
                                         e.dma_start(out=cmask[:, :], in_=dr["cmask"][:, :])],
              writes=["ident", "cmask"], n=2)
        t.dma("sp", "c_f32", lambda e: [e.dma_start(out=u32[:, :], in_=dr["u32"][:, :]),
                                        e.dma_start(out=ones32[:, :], in_=dr["ones32"][:, :]),
                                        e.dma_start(out=cc[:, :, :].rearrange("p a b -> p (a b)"), in_=dr["cc"][:, :]),
                                        e.dma_start(out=ssg[:, :, :].rearrange("p a b -> p (a b)"), in_=dr["ss"][:, :]),
                                        e.dma_start(out=slopes_bc, in_=dr["slopes"].partition_broadcast(128))],
              writes=["u32", "ones32", "cc", "ssg", "slopes"], n=5)
        t.op("pool", lambda e: e.memset(VA[:, :, :, 64:65], 1.0), writes=["VA"])
        for i in range(2):
            t.op("pool", lambda e, i=i: e.memset(ln_t[i][:, 640:768], 0.0), writes=[f"lnt{i}"])

    def load_cols(dst, src, ncols, slot, res):
        step = 2048
        pieces = [(c0, min(step, ncols - c0)) for c0 in range(0, ncols, step)]

        def fn(e):
            return [e.dma_start(out=dst[:, c0:c0 + n], in_=src[:, c0:c0 + n]) for c0, n in pieces]
        t.dma("pool", slot, fn, writes=[res], n=len(pieces))

    def load_wx(name):
        load_cols(WX, dr[name], 8192, "WX", "WX")

    def phase_A(L, src, src_res, kind):
        Wg = WX[:, :].rearrange("p (c n) -> p c n", c=8)
        if kind == 0:
            Wl = WB[:, M_WL:M_WQ].rearrange("p (c n) -> p c n", c=8)
            LT = Q32[:, 0:10240].rearrange("p (c n) -> p c n", c=5)
            KR = Q32[:, 10240:12288]
        xbring = Ring([(Q32[:, 12288 + i * 1024:12288 + (i + 1) * 1024], f"xb{i}") for i in range(2)])
        ebring = Ring([(Q32[:, 14336 + i * 1024:14336 + (i + 1) * 1024].bitcast(F32), f"eb{i}") for i in range(2)])
        def front(ti):
            xs, xs_r = xsring.next()
            t.dma("sp", xs_r, lambda e, xs=xs, ti=ti: e.dma_start(out=xs[:, :], in_=src[ti * 128:(ti + 1) * 128, :]),
                  reads=[f"{src_res}{ti}"], writes=[xs_r])
            xb_, xb_r = xbring.next()
            t.op("dve", lambda e, xs=xs, xb_=xb_: e.tensor_copy(out=xb_, in_=xs[:, :]),
                 reads=[xs_r], writes=[xb_r])
            tb, tb_r = tring.next()

            def tr8(e, tb=tb, xb_=xb_):
                for k in range(8):
                    ins = e.transpose(out=tb[:, k * 128:(k + 1) * 128], in_=xb_[:, k * 128:(k + 1) * 128],
                                      identity=ident[:, :])
                return ins
            t.op("pe", tr8, reads=[xb_r, "ident"], writes=[tb_r])
            t.op("dve", lambda e, tb=tb, ti=ti: e.tensor_copy(
                out=xT[:, :, ti * 128:(ti + 1) * 128],
                in_=tb[:, :].rearrange("p (c n) -> p c n", c=8)), reads=[tb_r],
                writes=[xT_res[ti], "xoR0h0", "xoR0h1", "xoR1h0", "xoR1h1"])

        def main(ti):
            KSUB = int(os.environ.get("KSUB", "9"))
            if KSUB < 2:
                return
            for half in range(2):
                ps, ps_r = short.next()

                def gmm(e, ps=ps, ti=ti, half=half):
                    for k in range(8):
                        ins = e.matmul(ps[:, :], lhsT=xT[:, k, ti * 128:(ti + 1) * 128],
                                       rhs=Wg[:, k, half * 512:(half + 1) * 512], start=(k == 0), stop=(k == 7))
                    return ins
                t.op("pe", gmm, reads=[xT_res[ti], "WX"], writes=[ps_r])
                gres = [f"G{ti}_{k}" for k in range(half * 4, half * 4 + 4)]
                if True:
                    t.op("act", lambda e, ps=ps, ti=ti, half=half: e.activation(
                        out=GB[:, ti, half * 512:(half + 1) * 512], in_=ps[:, :], func=AF.Silu),
                        reads=[ps_r], writes=gres)
                else:
                    eb, eb_r = ebring.next()
                    t.op("act", lambda e, ps=ps, eb=eb: e.activation(out=eb, in_=ps[:, :], func=AF.Exp, scale=-1.0),
                         reads=[ps_r], writes=[eb_r])
                    t.op("pool", lambda e, eb=eb: e.tensor_scalar(out=eb, in0=eb, scalar1=1.0, scalar2=None,
                                                                  op0=ALU.add), reads=[eb_r], writes=[eb_r])
                    t.op("dve", lambda e, eb=eb: e.reciprocal(out=eb, in_=eb), reads=[eb_r], writes=[eb_r])
                    t.op("dve", lambda e, ps=ps, eb=eb, ti=ti, half=half: e.tensor_tensor(
                        out=GB[:, ti, half * 512:(half + 1) * 512], in0=ps[:, :], in1=eb, op=ALU.mult),
                        reads=[ps_r, eb_r], writes=gres)
            return

        def lat(ti):
            KSUB = 9
            psA, psA_r = short.next()
            psB, psB_r = short.next()

            def lmm(e, psA=psA, psB=psB, ti=ti):
                for k in range(8):
                    e.matmul(psA[:, 0:416], lhsT=xT[:, k, ti * 128:(ti + 1) * 128], rhs=Wl[:, k, 0:416],
                             start=(k == 0), stop=(k == 7))
                for k in range(8):
                    ins = e.matmul(psB[:, 0:256], lhsT=xT[:, k, ti * 128:(ti + 1) * 128], rhs=Wl[:, k, 416:672],
                                   start=(k == 0), stop=(k == 7))
                return ins
            t.op("pe", lmm, reads=[xT_res[ti], "WB"], writes=[psA_r, psB_r])
            pr = ti % 2
            ms_ = small[:, 0:2] if pr == 0 else small[:, 182:184]
            lnms_ = small[:, 2:4] if pr == 0 else small[:, 184:186]
            rstd_ = small[:, 4:6] if pr == 0 else small[:, 186:188]
            rk_ = rk[:, pr * 64:(pr + 1) * 64]
            t.op("act", lambda e, psA=psA, ms_=ms_: e.activation(out=junk[:, 0:384], in_=psA[:, 0:384], func=AF.Square,
                                                                   scale=float(384 ** -0.5), accum_out=ms_[:, 0:1]),
                 reads=[psA_r], writes=[f"ms0_{pr}", "junk"])
            t.op("act", lambda e, psB=psB, ms_=ms_: e.activation(out=junk[:, 0:256], in_=psB[:, 0:256], func=AF.Square,
                                                                   scale=float(256 ** -0.5), accum_out=ms_[:, 1:2]),
                 reads=[psB_r], writes=[f"ms1_{pr}", "junk"])
            if KSUB < 4:
                return
            t.op("act", lambda e, ms_=ms_, lnms_=lnms_: e.activation(out=lnms_, in_=ms_, func=AF.Ln, bias=RMS_EPS,
                                                                       scale=1.0),
                 reads=[f"ms0_{pr}", f"ms1_{pr}"], writes=[f"lnms{pr}"])
            t.op("act", lambda e, lnms_=lnms_, rstd_=rstd_: e.activation(out=rstd_, in_=lnms_, func=AF.Exp, scale=-0.5),
                 reads=[f"lnms{pr}"], writes=[f"rstd{pr}"])
            if KSUB < 5:
                return
            lt, lt_r = lnring.next()
            t.op("dve", lambda e, psA=psA, lt=lt, rstd_=rstd_: e.scalar_tensor_tensor(
                out=lt[:, 0:384], in0=psA[:, 0:384], scalar=rstd_[:, 0:1], in1=gq[:, 0:384],
                op0=ALU.mult, op1=ALU.mult), reads=[psA_r, f"rstd{pr}", "gq"], writes=[lt_r + "a"])
            t.op("dve", lambda e, psB=psB, lt=lt, rstd_=rstd_: e.scalar_tensor_tensor(
                out=lt[:, 384:640], in0=psB[:, 0:256], scalar=rstd_[:, 1:2], in1=gq[:, 384:640],
                op0=ALU.mult, op1=ALU.mult), reads=[psB_r, f"rstd{pr}", "gq"], writes=[lt_r + "b"])
            t.op("dve", lambda e, psA=psA, ti=ti, rk_=rk_: e.tensor_tensor(out=rk_[:, 0:32], in0=psA[:, 384:416],
                                                                            in1=cc[:, ti, :], op=ALU.mult),
                 reads=[psA_r, "cc"], writes=[f"rk1_{pr}"])
            t.op("dve", lambda e, psA=psA, ti=ti, rk_=rk_: e.tensor_tensor(out=rk_[:, 32:48], in0=psA[:, 400:416],
                                                                            in1=ssg[:, ti, 0:16], op=ALU.mult),
                 reads=[psA_r, "ssg"], writes=[f"rk2a_{pr}"])
            t.op("dve", lambda e, psA=psA, ti=ti, rk_=rk_: e.tensor_tensor(out=rk_[:, 48:64], in0=psA[:, 384:400],
                                                                            in1=ssg[:, ti, 16:32], op=ALU.mult),
                 reads=[psA_r, "ssg"], writes=[f"rk2b_{pr}"])
            t.op("pool", lambda e, lt=lt, rk_=rk_: e.tensor_tensor(out=lt[:, 704:736], in0=rk_[:, 0:32],
                                                                  in1=rk_[:, 32:64], op=ALU.add),
                 reads=[f"rk1_{pr}", f"rk2a_{pr}", f"rk2b_{pr}"], writes=[lt_r + "c"])
            return lt, lt_r

        def tail(ti, lt, lt_r):
            tb, tb_r = tring.next()

            def tr6(e, tb=tb, lt=lt):
                for c in range(6):
                    ins = e.transpose(out=tb[:, c * 128:(c + 1) * 128], in_=lt[:, c * 128:(c + 1) * 128],
                                      identity=ident[:, :])
                return ins
            t.op("pe", tr6, reads=[lt_r, lt_r + "a", lt_r + "b", lt_r + "c", "ident"],
                 writes=[tb_r])
            t.op("dve", lambda e, tb=tb, ti=ti: e.tensor_copy(
                out=LT[:, :, ti * 128:(ti + 1) * 128], in_=tb[:, 0:640].rearrange("p (c n) -> p c n", c=5)),
                reads=[tb_r], writes=[f"LT{ti}"])
            t.op("dve", lambda e, tb=tb, ti=ti: e.tensor_copy(
                out=KR[64:96, ti * 128:(ti + 1) * 128], in_=tb[64:96, 640:768]),
                reads=[tb_r], writes=["KR"])

        front(0)
        front(1)
        pend = None
        for ti in range(NT):
            main(ti)
            if ti + 2 < NT:
                front(ti + 2)
            if kind == 0:
                r = lat(ti)
                if pend is not None:
                    tail(*pend)
                pend = (ti,) + tuple(r)
        if pend is not None:
            tail(*pend)

    def attn_out(h, G, po, po_r, extra_den=None):
        rc, rc_r = rcring.next()
        tmp, tmp_r = tmpring.next()
        pov = po[:, 0:260].rearrange("p (i c) -> p i c", c=65)
        if extra_den is None:
            t.op("dve", lambda e: e.reciprocal(out=rc, in_=pov[:, :, 64]), reads=[po_r], writes=[rc_r])
        else:
            t.op("dve", lambda e: e.tensor_scalar(out=den_t, in0=pov[:, :, 64], scalar1=extra_den, scalar2=None,
                                                  op0=ALU.add), reads=[po_r, "esink"], writes=["den"])
            t.op("dve", lambda e: e.reciprocal(out=rc, in_=den_t), reads=["den"], writes=[rc_r])
        t.op("dve", lambda e: e.tensor_tensor(out=tmp[:, :, :], in0=pov[:, :, 0:64],
                                              in1=rc.unsqueeze(2).broadcast_to([128, 4, 64]), op=ALU.mult),
             reads=[po_r, rc_r], writes=[tmp_r])
        gv = GB[:, 4 * G:4 * G + 4, h * 64:(h + 1) * 64]
        gres = [f"G{ti}_{h // 2}" for ti in range(4 * G, 4 * G + 4)]
        t.op("dve", lambda e: e.tensor_tensor(out=gv, in0=tmp[:, :, :], in1=gv, op=ALU.mult),
             reads=[tmp_r] + gres, writes=gres)

    LA = 2
    pvq = []

    def push(fns):
        pvq.append(fns)
        if len(pvq) > LA:
            for f in pvq.pop(0):
                f()

    def flush():
        while pvq:
            for f in pvq.pop(0):
                f()

    def attn_dense(h, hl, khl, kd, scale, QT, KT, bias_fn, kextra):
        qres, kres = f"QT{hl}", f"KT{khl}"
        for G in range(NGRP):
            po, po_r = longr.next()
            first = [True]
            prev = None

            def pv(prev, po=po, po_r=po_r, first=first):
                j, pt, pt_r, r = prev
                items = []
                for i in range(max(r, 0), 4):
                    items.append((i, first[0]))
                    first[0] = False

                def fn(e):
                    ins = None
                    for i, st in items:
                        ins = e.matmul(po[:, i * 65:(i + 1) * 65], lhsT=pt[:, i * 128:(i + 1) * 128],
                                       rhs=VA[:, j, khl, :], start=st, stop=False, skip_group_check=True)
                    return ins
                t.op("pe", fn, reads=[pt_r, "VA"], writes=[po_r])

            for j in range(4 * G + 4):
                ps, ps_r = short.next()
                pt, pt_r = ptring.next()
                r = j - 4 * G
                c0 = max(r, 0) * 128
                kt = KT[0:kd, khl, j * 128:(j + 1) * 128]

                def qk(e, ps=ps, r=r, c0=c0, kt=kt, G=G):
                    if r < 0:
                        return e.matmul(ps[:, :], lhsT=kt, rhs=QT[0:kd, hl, G * 512:(G + 1) * 512],
                                        start=True, stop=True)
                    e.matmul(ps[:, c0:c0 + 128], lhsT=kt, rhs=QT[0:kd, hl, G * 512 + c0:G * 512 + c0 + 128],
                             start=True, stop=False, skip_group_check=True)
                    ins = e.matmul(ps[:, c0:c0 + 128], lhsT=ident[:, :], rhs=cmask[:, :], start=False, stop=True,
                                   skip_group_check=True)
                    if c0 + 128 < 512:
                        ins = e.matmul(ps[:, c0 + 128:512], lhsT=kt,
                                       rhs=QT[0:kd, hl, G * 512 + c0 + 128:(G + 1) * 512],
                                       start=False, stop=True, skip_group_check=True)
                    return ins
                t.op("pe", qk, reads=[qres, kres, "ident", "cmask"] + kextra, writes=[ps_r])
                b_ap, b_res = bias_fn(j)
                t.op("act", lambda e, ps=ps, pt=pt, c0=c0, b_ap=b_ap: e.activation(
                    out=pt[:, c0:512], in_=ps[:, c0:512], func=AF.Exp, scale=scale, bias=b_ap),
                    reads=[ps_r] + b_res, writes=[pt_r])
                fns = [lambda pv=pv, a=(j, pt, pt_r, r): pv(a)]
                if j == 4 * G + 3:
                    fns.append(lambda h=h, G=G, po=po, po_r=po_r: attn_out(h, G, po, po_r))
                push(fns)

    def attn_band(h, hl, QT, KT):
        qres, kres = f"QT{hl}", "KT0"
        for G in range(NGRP):
            po, po_r = longr.next()
            first = [True]
            for pp in range(2):
                i0 = 4 * G + 2 * pp
                ps, ps_r = short.next()
                pt, pt_r = ptring.next()

                def qk(e, ps=ps, i0=i0):
                    ins = None
                    st = True
                    for tt in range(2):
                        i = i0 + tt
                        jprev = max(i - 1, 0)
                        for jj, j in enumerate((jprev, i)):
                            col = (tt * 2 + jj) * 128
                            ins = e.matmul(ps[:, col:col + 128], lhsT=KT[0:64, 0, j * 128:(j + 1) * 128],
                                           rhs=QT[0:64, hl, i * 128:(i + 1) * 128], start=st, stop=False,
                                           skip_group_check=True)
                            st = False
                    return ins
                t.op("pe", qk, reads=[qres, kres], writes=[ps_r])
                dm = dm4f if i0 == 0 else dm4
                swt, swt_r = swring.next()
                t.op("dve", lambda e, ps=ps, dm=dm, swt=swt: e.scalar_tensor_tensor(
                    out=swt, in0=dm, scalar=slopes_bc[:, h:h + 1], in1=ps[:, :],
                    op0=ALU.mult, op1=ALU.add), reads=[ps_r, "dm4", "dm4f", "slopes"], writes=[swt_r])
                t.op("act", lambda e, pt=pt, swt=swt: e.activation(out=pt[:, :], in_=swt, func=AF.Exp, scale=0.125),
                     reads=[swt_r], writes=[pt_r])

                items = []
                for tt in range(2):
                    i = i0 + tt
                    oc = (pp * 2 + tt) * 65
                    for jj, j in enumerate((i - 1, i)):
                        if j < 0:
                            continue
                        items.append((oc, (tt * 2 + jj) * 128, j, first[0]))
                        first[0] = False

                def pv(e, pt=pt, items=items, po=po):
                    ins = None
                    for oc, col, j, st in items:
                        ins = e.matmul(po[:, oc:oc + 65], lhsT=pt[:, col:col + 128], rhs=VA[:, j, 0, :],
                                       start=st, stop=False, skip_group_check=True)
                    return ins
                fns = [lambda pv=pv, pt_r=pt_r, po_r=po_r: t.op("pe", pv, reads=[pt_r, "VA"], writes=[po_r])]
                if pp == 1:
                    fns.append(lambda h=h, G=G, po=po, po_r=po_r: attn_out(h, G, po, po_r,
                                                                          extra_den=esink[:, h:h + 1]))
                push(fns)

    def proj_fm(dst, rows, chunks, lhs_fn, rhs_fn, reads, writes, extra=None, eng=None):
        ps, ps_r = proj6.next()

        def fn(e):
            ins = None
            for ci, c in enumerate(chunks):
                ins = e.matmul(ps[0:rows, :], lhsT=lhs_fn(c), rhs=rhs_fn(c), start=(ci == 0),
                               stop=(ci == len(chunks) - 1))
            if extra is not None:
                ins = extra(e, ps)
            return ins
        t.op("pe", fn, reads=reads, writes=[ps_r])
        copy_op(eng or evac_engine(), dst, ps[0:rows, :], [ps_r], writes)

    def mixer_mla(L):
        t.barrier()
        QT = R32[:, 0:8192].rearrange("p (h n) -> p h n", h=4)
        KT = R32[:, 8192:16384].rearrange("p (h n) -> p h n", h=4)
        LT = Q32[:, 0:10240].rearrange("p (c n) -> p c n", c=5)
        KR = Q32[:, 10240:12288]
        T1 = [Q32[:, 12288 + i * 1024:12288 + (i + 1) * 1024].bitcast(F32) for i in range(2)]
        T2 = [Q32[:, 14336 + i * 1024:14336 + (i + 1) * 1024].bitcast(F32) for i in range(2)]
        t1r = Ring([(T1[i], f"T1_{i}") for i in range(2)])
        t2r = Ring([(T2[i], f"T2_{i}") for i in range(2)])
        Wq = WB[:, M_WQ:M_WQS].rearrange("p (c n) -> p c n", c=3)
        Wqs = WB[:, M_WQS:M_WKK].rearrange("p (c n) -> p c n", c=3)
        Wkk = WB[:, M_WKK:M_WKV].rearrange("p (c n) -> p c n", c=2)
        Wkv = WB[:, M_WKV:M_END].rearrange("p (c n) -> p c n", c=2)
        cosF = X16[:, 0:2048]
        sinF = X16[:, 2048:4096]
        ltall = [f"LT{i}" for i in range(NT)]
        t.op("pool", lambda e: e.tensor_copy(out=KT[64:96, :, :],
                                             in_=KR[64:96, :].unsqueeze(1).broadcast_to([32, 4, S])),
             reads=["KR"], writes=["KTrope"])
        scale = float(96 ** -0.5)
        for g in range(4):
            for hl in range(4):
                h = 4 * g + hl
                for G in range(NGRP):
                    cs = slice(G * 512, (G + 1) * 512)
                    lt_reads = ltall[4 * G:4 * G + 4]
                    proj_fm(KT[0:64, hl, cs], 64, [0, 1],
                            lambda c, h=h: Wkk[:, c, h * 64:(h + 1) * 64],
                            lambda c, cs=cs: LT[:, 3 + c, cs], lt_reads + ["WB"], [f"KT{hl}"], eng="act")
                    ps1, ps1_r = proj6.next()
                    ps2, ps2_r = proj6.next()

                    def qmm(e, ps1=ps1, ps2=ps2, h=h, cs=cs):
                        for c in range(3):
                            e.matmul(ps1[0:96, :], lhsT=Wq[:, c, h * 96:(h + 1) * 96], rhs=LT[:, c, cs],
                                     start=(c == 0), stop=(c == 2))
                        for c in range(3):
                            ins = e.matmul(ps2[0:96, :], lhsT=Wqs[:, c, h * 32:h * 32 + 96], rhs=LT[:, c, cs],
                                           start=(c == 0), stop=(c == 2))
                        return ins
                    t.op("pe", qmm, reads=lt_reads + ["WB"], writes=[ps1_r, ps2_r])
                    t.op("act", lambda e, ps1=ps1, hl=hl, cs=cs: e.activation(
                        out=QT[0:64, hl, cs], in_=ps1[0:64, :], func=AF.Copy), reads=[ps1_r], writes=[f"QT{hl}"])
                    a1, a1_r = t1r.next()
                    a2, a2_r = t2r.next()
                    t.op("dve", lambda e, ps1=ps1, a1=a1, cs=cs: e.tensor_tensor(
                        out=a1[64:96, :], in0=ps1[64:96, :], in1=cosF[64:96, cs], op=ALU.mult),
                        reads=[ps1_r, "X16"], writes=[a1_r])
                    t.op("dve", lambda e, ps2=ps2, a2=a2, cs=cs: e.tensor_tensor(
                        out=a2[64:96, :], in0=ps2[64:96, :], in1=sinF[64:96, cs], op=ALU.mult),
                        reads=[ps2_r, "X16"], writes=[a2_r])
                    t.op("dve", lambda e, a1=a1, a2=a2, hl=hl, cs=cs: e.tensor_tensor(
                        out=QT[64:96, hl, cs], in0=a1[64:96, :], in1=a2[64:96, :], op=ALU.add),
                        reads=[a1_r, a2_r], writes=[f"QT{hl}"])
            for ti in range(NT):
                ps, ps_r = short.next()

                def vmm(e, ps=ps, ti=ti, g=g):
                    for c in range(2):
                        ins = e.matmul(ps[:, 0:256], lhsT=LT[:, 3 + c, ti * 128:(ti + 1) * 128],
                                       rhs=Wkv[:, c, g * 256:(g + 1) * 256], start=(c == 0), stop=(c == 1))
                    return ins
                t.op("pe", vmm, reads=[f"LT{ti}", "WB"], writes=[ps_r])
                copy_op("dve", VA[:, ti, :, 0:64], ps[:, 0:256].rearrange("p (h d) -> p h d", h=4), [ps_r], ["VA"])
            for hl in range(4):
                attn_dense(4 * g + hl, hl, hl, 96, scale, QT, KT, lambda j: (0.0, []), ["KTrope"])
            flush()

    def load_group(L, g, nel):
        slot = g % 2
        load_cols(WB[:, slot * SLOT:slot * SLOT + nel], dr[f"wb{L}"][g], nel, f"WBs{slot}", f"WBs{slot}")

    def mixer_swa(L):
        QT = Q32[:, 0:8192].rearrange("p (h n) -> p h n", h=4)
        KT = Q32[:, 8192:16384].rearrange("p (h n) -> p h n", h=4)
        t.dma("sp", "sinks", lambda e: e.dma_start(out=sinks_bc, in_=dr[f"sinks{L}"].partition_broadcast(128)),
              writes=["sinks"])
        t.dma("sp", "X16", lambda e: [e.dma_start(out=dm4, in_=dr["dm4"][:, :]),
                                      e.dma_start(out=dm4f, in_=dr["dm4f"][:, :])], writes=X16_ALL, n=2)
        t.op("act", lambda e: e.activation(out=esink, in_=sinks_bc, func=AF.Exp), reads=["sinks"], writes=["esink"])
        for g in range(4):
            slot = g % 2
            if g + 1 < 4:
                load_group(L, g + 1, W_END)
            base = slot * SLOT
            Wq = WB[:, base + W_WQ:base + W_WK].rearrange("p (c n) -> p c n", c=8)
            Wk = WB[:, base + W_WK:base + W_WV].rearrange("p (c n) -> p c n", c=8)
            Wv = WB[:, base + W_WV:base + W_END].rearrange("p (c n) -> p c n", c=8)
            wres = f"WBs{slot}"
            pend = None

            def tpose(ti, qt_, qt_r, kt_, kt_r):
                tb, tb_r = tring.next()

                def trq(e, tb=tb, qt_=qt_, kt_=kt_):
                    for hl in range(4):
                        e.transpose(out=tb[0:64, hl * 128:(hl + 1) * 128], in_=qt_[:, hl * 64:(hl + 1) * 64],
                                    identity=ident[:, :])
                    return e.transpose(out=tb[0:64, 512:640], in_=kt_[:, 0:64], identity=ident[:, :])
                t.op("pe", trq, reads=[qt_r, kt_r, "ident"], writes=[tb_r])
                t.op("dve", lambda e, tb=tb, ti=ti: e.tensor_copy(
                    out=QT[0:64, :, ti * 128:(ti + 1) * 128],
                    in_=tb[0:64, 0:512].rearrange("p (h n) -> p h n", h=4)),
                    reads=[tb_r], writes=[f"QT{i}" for i in range(4)])
                t.op("dve", lambda e, tb=tb, ti=ti: e.tensor_copy(
                    out=KT[0:64, 0, ti * 128:(ti + 1) * 128], in_=tb[0:64, 512:640]),
                    reads=[tb_r], writes=["KT0"])

            for ti in range(NT):
                psq, psq_r = proj6.next()
                psk, psk_r = proj6.next()
                psv, psv_r = proj6.next()

                def pmm(e, psq=psq, psk=psk, psv=psv, ti=ti, Wq=Wq, Wk=Wk, Wv=Wv):
                    ins = None
                    for c in range(8):
                        e.matmul(psq[:, 0:256], lhsT=xT[:, c, ti * 128:(ti + 1) * 128], rhs=Wq[:, c, :],
                                 start=(c == 0), stop=(c == 7))
                    for c in range(8):
                        e.matmul(psk[:, 0:64], lhsT=xT[:, c, ti * 128:(ti + 1) * 128], rhs=Wk[:, c, :],
                                 start=(c == 0), stop=(c == 7))
                    for c in range(8):
                        ins = e.matmul(psv[:, 0:64], lhsT=xT[:, c, ti * 128:(ti + 1) * 128], rhs=Wv[:, c, :],
                                       start=(c == 0), stop=(c == 7))
                    return ins
                t.op("pe", pmm, reads=[xT_res[ti], wres], writes=[psq_r, psk_r, psv_r])
                qt_, qt_r = qtring.next()
                kt_, kt_r = ktring.next()
                t.op("dve", lambda e, psq=psq, qt_=qt_: e.tensor_copy(out=qt_[:, 0:256], in_=psq[:, 0:256]),
                     reads=[psq_r], writes=[qt_r])
                t.op("dve", lambda e, psk=psk, kt_=kt_: e.tensor_copy(out=kt_[:, 0:64], in_=psk[:, 0:64]),
                     reads=[psk_r], writes=[kt_r])
                copy_op("dve", VA[:, ti, 0, 0:64], psv[:, 0:64], [psv_r], ["VA"])
                if pend is not None:
                    tpose(*pend)
                pend = (ti, qt_, qt_r, kt_, kt_r)
            tpose(*pend)
            for hl in range(4):
                attn_band(4 * g + hl, hl, QT, KT)
            flush()

    def mixer_fox(L):
        QT = Q32[:, 0:8192].rearrange("p (h n) -> p h n", h=4)
        KT = Q32[:, 8192:16384].rearrange("p (h n) -> p h n", h=4)
        Zg = X16[:, 0:2080].bitcast(BF16).rearrange("p (t h c) -> p t h c", t=NT, h=4)
        nlf = X16[:, 2080:2336].rearrange("p (t h) -> p t h", t=NT)
        negck = X16[:, 2336:2592].rearrange("p (t h) -> p t h", t=NT)
        cum8 = X16[:, 2592:2720].bitcast(BF16).rearrange("p (t h) -> p t h", t=NT)
        Wf = X16[:, 2720:2784].bitcast(BF16).rearrange("p (c n) -> p c n", c=8)
        Zg = None
        t.dma("pool", "wf", lambda e: e.dma_start(out=X16[:, 2720:2784].bitcast(BF16), in_=dr[f"wf{L}"][:, :]),
              writes=X16_ALL)
        t.dma("sp", "bf", lambda e: e.dma_start(out=bf_bc, in_=dr[f"bf{L}"].partition_broadcast(128)),
              writes=["bf"])
        t.op("pool", lambda e: e.memset(X16[:, 0:2080], 0.0), writes=X16_ALL)
        t.op("pool", lambda e: e.memset(carry[0], 0.0), writes=["carry0"])
        t.op("pool", lambda e: e.memset(KT[64:65, :, :], 1.0), writes=["KTrope"])
        for ti in range(NT):
            ps, ps_r = short.next()

            def fmm(e, ps=ps, ti=ti):
                for c in range(8):
                    ins = e.matmul(ps[:, 0:16], lhsT=xT[:, c, ti * 128:(ti + 1) * 128], rhs=Wf[:, c, :],
                                   start=(c == 0), stop=(c == 7))
                return ins
            t.op("pe", fmm, reads=[xT_res[ti], "Wf"], writes=[ps_r])
            t.op("dve", lambda e, ps=ps: e.tensor_tensor(out=zt, in0=ps[:, 0:16], in1=bf_bc, op=ALU.add),
                 reads=[ps_r, "bf"], writes=["zt"])
            t.op("act", lambda e: e.activation(out=et, in_=zt, func=AF.Exp, scale=-1.0), reads=["zt"], writes=["et"])
            t.op("act", lambda e, ti=ti: e.activation(out=nlf[:, ti, :], in_=et, func=AF.Ln, bias=1.0, scale=1.0),
                 reads=["et"], writes=[f"nlf{ti}"])
            ps2, ps2_r = short.next()

            def cmm(e, ps2=ps2, ti=ti):
                e.matmul(ps2[:, 0:16], lhsT=u32[:, :], rhs=nlf[:, ti, :], start=True, stop=False,
                         skip_group_check=True)
                return e.matmul(ps2[:, 16:32], lhsT=ones32[:, :], rhs=nlf[:, ti, :], start=False, stop=True,
                                skip_group_check=True)
            t.op("pe", cmm, reads=[f"nlf{ti}", "u32", "ones32"], writes=[ps2_r])
            ca, cb = carry[ti % 2], carry[(ti + 1) % 2]
            car, cbr = f"carry{ti % 2}", f"carry{(ti + 1) % 2}"
            t.op("dve", lambda e, ps2=ps2, ti=ti, ca=ca: e.tensor_tensor(out=negck[:, ti, :], in0=ps2[:, 0:16],
                                                                          in1=ca, op=ALU.add),
                 reads=[ps2_r, car], writes=[f"negck{ti}"])
            t.op("dve", lambda e, ps2=ps2, ca=ca, cb=cb: e.tensor_tensor(out=cb, in0=ps2[:, 16:32], in1=ca,
                                                                          op=ALU.add),
                 reads=[ps2_r, car], writes=[cbr])
            t.op("pool", lambda e, ti=ti: e.tensor_scalar(out=cum8[:, ti, :], in0=negck[:, ti, :], scalar1=-8.0,
                                                          scalar2=None, op0=ALU.mult),
                 reads=[f"negck{ti}"], writes=["cum8"])
        nk_all = [f"negck{ti}" for ti in range(NT)]
        for g in range(4):
            slot = g % 2
            if g + 1 < 4:
                load_group(L, g + 1, F_END)
            base = slot * SLOT
            Wq = WB[:, base + F_WQ:base + F_WK].rearrange("p (c n) -> p c n", c=8)
            Wk = WB[:, base + F_WK:base + F_WV].rearrange("p (c n) -> p c n", c=8)
            Wv = WB[:, base + F_WV:base + F_END].rearrange("p (c n) -> p c n", c=8)
            wres = f"WBs{slot}"
            pend = None

            def tpose(ti, qt_, qt_r, kt_, kt_r):
                tb, tb_r = tring.next()

                def trq(e, tb=tb, qt_=qt_, kt_=kt_):
                    ins = None
                    for hl in range(4):
                        e.transpose(out=tb[0:65, hl * 128:(hl + 1) * 128], in_=qt_[:, hl * 65:(hl + 1) * 65],
                                    identity=ident[:, :])
                        ins = e.transpose(out=tb[0:64, 512 + hl * 128:512 + (hl + 1) * 128],
                                          in_=kt_[:, hl * 64:(hl + 1) * 64], identity=ident[:, :])
                    return ins
                t.op("pe", trq, reads=[qt_r, qt_r + "c", kt_r, "ident"], writes=[tb_r])
                t.op("dve", lambda e, tb=tb, ti=ti: e.tensor_copy(
                    out=QT[0:65, :, ti * 128:(ti + 1) * 128],
                    in_=tb[0:65, 0:512].rearrange("p (h n) -> p h n", h=4)),
                    reads=[tb_r], writes=[f"QT{i}" for i in range(4)])
                t.op("dve", lambda e, tb=tb, ti=ti: e.tensor_copy(
                    out=KT[0:64, :, ti * 128:(ti + 1) * 128],
                    in_=tb[0:64, 512:1024].rearrange("p (h n) -> p h n", h=4)),
                    reads=[tb_r], writes=[f"KT{i}" for i in range(4)])

            for ti in range(NT):
                psq, psq_r = proj6.next()
                psk, psk_r = proj6.next()
                psv, psv_r = proj6.next()

                def pmm(e, psq=psq, psk=psk, psv=psv, ti=ti, Wq=Wq, Wk=Wk, Wv=Wv):
                    ins = None
                    for ps, W in ((psq, Wq), (psk, Wk), (psv, Wv)):
                        for c in range(8):
                            ins = e.matmul(ps[:, 0:256], lhsT=xT[:, c, ti * 128:(ti + 1) * 128], rhs=W[:, c, :],
                                           start=(c == 0), stop=(c == 7))
                    return ins
                t.op("pe", pmm, reads=[xT_res[ti], wres], writes=[psq_r, psk_r, psv_r])
                qt_, qt_r = qtring.next()
                kt_, kt_r = ktring.next()
                t.op("dve", lambda e, psq=psq, qt_=qt_: e.tensor_copy(
                    out=qt_[:, :].rearrange("p (h c) -> p h c", h=4)[:, :, 0:64],
                    in_=psq[:, 0:256].rearrange("p (h d) -> p h d", h=4)), reads=[psq_r], writes=[qt_r])
                t.op("pool", lambda e, qt_=qt_, ti=ti, g=g: e.tensor_copy(
                    out=qt_[:, :].rearrange("p (h c) -> p h c", h=4)[:, :, 64], in_=cum8[:, ti, 4 * g:4 * g + 4]),
                    reads=["cum8"], writes=[qt_r + "c"])
                t.op("act", lambda e, psk=psk, kt_=kt_: e.activation(out=kt_[:, :], in_=psk[:, 0:256], func=AF.Copy),
                     reads=[psk_r], writes=[kt_r])
                copy_op("dve", VA[:, ti, :, 0:64], psv[:, 0:256].rearrange("p (h d) -> p h d", h=4), [psv_r], ["VA"])
                if pend is not None:
                    tpose(*pend)
                pend = (ti, qt_, qt_r, kt_, kt_r)
            tpose(*pend)
            for hl in range(4):
                h = 4 * g + hl
                attn_dense(h, hl, hl, 65, 0.125, QT, KT,
                           lambda j, h=h: (negck[:, j, h:h + 1], [f"negck{j}"]), ["KTrope"])
            flush()

    def phase_C(L, src, src_res, dst, dst_res):
        Wo = WX[:, :].rearrange("p (c n) -> p c n", c=8)
        lng = X16[:, 0:1024]
        lnb = X16[:, 1024:2048]
        RC = R32[:, :].bitcast(F32)
        sqjunk = Q32[:, 0:1024]
        xcring = Ring([(RC[:, i * 1024:(i + 1) * 1024], f"xc{i}") for i in range(4)])
        zring = Ring([(RC[:, 4096 + i * 1024:4096 + (i + 1) * 1024], f"zb{i}") for i in range(2)])
        xoring = Ring([(RC[:, 6144 + i * 1024:6144 + (i + 1) * 1024], f"xoR{i}") for i in range(2)])
        sets = [dict(stats=small[:, 144:156], mv=small[:, 156:158], lnr=small[:, 6:8], nmr=small[:, 158:159]),
                dict(stats=small[:, 164:176], mv=small[:, 176:178], lnr=small[:, 178:180], nmr=small[:, 180:181])]
        t.dma("sp", "lngb", lambda e: [e.dma_start(out=lng, in_=dr["lng"][L].partition_broadcast(128)),
                                       e.dma_start(out=lnb, in_=dr["lnb"][L].partition_broadcast(128))],
              writes=X16_ALL, n=2)
        def front(ti):
            xs, xs_r = xcring.next()
            t.dma("sp", xs_r, lambda e, xs=xs, ti=ti: e.dma_start(out=xs, in_=src[ti * 128:(ti + 1) * 128, :]),
                  reads=[f"{src_res}{ti}"], writes=[xs_r])
            tb, tb_r = tring.next()

            def tr8(e, tb=tb, ti=ti):
                for k in range(8):
                    ins = e.transpose(out=tb[:, k * 128:(k + 1) * 128], in_=GB[:, ti, k * 128:(k + 1) * 128],
                                      identity=ident[:, :])
                return ins
            t.op("pe", tr8, reads=[f"G{ti}_{k}" for k in range(8)] + ["ident"], writes=[tb_r])
            og, og_r = ogring.next()
            t.op("act", lambda e, tb=tb, og=og: e.activation(out=og[:, :], in_=tb[:, :], func=AF.Copy),
                 reads=[tb_r], writes=[og_r])
            return xs, xs_r, og, og_r

        def rest(ti, xs, xs_r, og, og_r):
            ogv = og[:, :].rearrange("p (c n) -> p c n", c=8)
            pss = []
            for half in range(2):
                ps, ps_r = short.next()
                pss.append((ps, ps_r))

                def ymm(e, ps=ps, ogv=ogv, half=half):
                    for k in range(8):
                        ins = e.matmul(ps[:, :], lhsT=ogv[:, k, :], rhs=Wo[:, k, half * 512:(half + 1) * 512],
                                       start=(k == 0), stop=(k == 7))
                    return ins
                t.op("pe", ymm, reads=[og_r, "WX"], writes=[ps_r])
            zb, zb_r = zring.next()
            st = sets[ti % 2]
            sn = f"s{ti % 2}"
            stats_, mv_, lnr_, nmr_ = st["stats"], st["mv"], st["lnr"], st["nmr"]
            for half in range(2):
                ps, ps_r = pss[half]
                hs = slice(half * 512, (half + 1) * 512)
                t.op("dve", lambda e, ps=ps, xs=xs, hs=hs, zb=zb, half=half, stats_=stats_: e.scalar_tensor_tensor(
                    out=zb[:, hs], in0=xs[:, hs], scalar=ALPHA, in1=ps[:, :], op0=ALU.mult, op1=ALU.add,
                    accum_out=stats_[:, half:half + 1]),
                    reads=[ps_r, xs_r], writes=[f"{zb_r}h{half}", f"st{half}{sn}"])
            t.op("act", lambda e, zb=zb, stats_=stats_: e.activation(out=sqjunk, in_=zb, func=AF.Square,
                                                                      accum_out=stats_[:, 2:3]),
                 reads=[f"{zb_r}h0", f"{zb_r}h1"], writes=["sqjunk", f"st2{sn}"])
            t.op("dve", lambda e, mv_=mv_, stats_=stats_: e.tensor_scalar(
                out=mv_[:, 0:1], in0=stats_[:, 0:1], scalar1=stats_[:, 1:2], scalar2=1.0 / D,
                op0=ALU.add, op1=ALU.mult), reads=[f"st0{sn}", f"st1{sn}"], writes=[f"mean{sn}"])
            t.op("dve", lambda e, mv_=mv_, stats_=stats_: e.tensor_tensor(
                out=stats_[:, 3:4], in0=mv_[:, 0:1], in1=mv_[:, 0:1], op=ALU.mult),
                reads=[f"mean{sn}"], writes=[f"m2{sn}"])
            t.op("dve", lambda e, mv_=mv_, stats_=stats_: e.scalar_tensor_tensor(
                out=mv_[:, 1:2], in0=stats_[:, 2:3], scalar=1.0 / D, in1=stats_[:, 3:4],
                op0=ALU.mult, op1=ALU.subtract), reads=[f"st2{sn}", f"m2{sn}"], writes=[f"mv{sn}"])
            t.op("act", lambda e, mv_=mv_, lnr_=lnr_: e.activation(out=lnr_[:, 0:1], in_=mv_[:, 1:2], func=AF.Ln,
                                                                   bias=LN_EPS, scale=1.0),
                 reads=[f"mv{sn}"], writes=[f"lnr0{sn}"])
            t.op("act", lambda e, lnr_=lnr_: e.activation(out=lnr_[:, 1:2], in_=lnr_[:, 0:1], func=AF.Exp, scale=-0.5),
                 reads=[f"lnr0{sn}"], writes=[f"lnr1{sn}"])
            t.op("dve", lambda e, mv_=mv_, lnr_=lnr_, nmr_=nmr_: e.scalar_tensor_tensor(
                out=nmr_, in0=mv_[:, 0:1], scalar=-1.0, in1=lnr_[:, 1:2], op0=ALU.mult, op1=ALU.mult),
                reads=[f"mean{sn}", f"lnr1{sn}"], writes=[f"nmr{sn}"])
            t.op("act", lambda e, zb=zb, lnr_=lnr_, nmr_=nmr_: e.activation(
                out=zb, in_=zb, func=AF.Identity, bias=nmr_, scale=lnr_[:, 1:2]),
                reads=[f"{zb_r}h0", f"{zb_r}h1", f"lnr1{sn}", f"nmr{sn}"], writes=[f"{zb_r}h0", f"{zb_r}h1"])
            xo, xo_r = xoring.next()
            for half, eng in ((0, "dve"), (1, "dve")):
                hs = slice(half * 512, (half + 1) * 512)
                t.op(eng, lambda e, zb=zb, xo=xo, hs=hs: e.tensor_tensor(out=xo[:, hs], in0=zb[:, hs], in1=lng[:, hs],
                                                                         op=ALU.mult),
                     reads=[f"{zb_r}h{half}", "lng"], writes=[f"{xo_r}h{half}"])
                t.op(eng, lambda e, xo=xo, hs=hs: e.tensor_tensor(out=xo[:, hs], in0=xo[:, hs], in1=lnb[:, hs],
                                                                  op=ALU.add),
                     reads=[f"{xo_r}h{half}", "lnb"], writes=[f"{xo_r}h{half}"])
            t.dma("sp", xo_r, lambda e, ti=ti, xo=xo: e.dma_start(out=dst[ti * 128:(ti + 1) * 128, :], in_=xo),
                  reads=[f"{xo_r}h0", f"{xo_r}h1"], writes=[f"{dst_res}{ti}"])

        ctxs = {0: front(0), 1: front(1)}
        for ti in range(NT):
            if ti + 2 < NT:
                ctxs[ti + 2] = front(ti + 2)
            rest(ti, *ctxs.pop(ti))

    def load_b_weights(L):
        k = KINDS[L]
        if k == 0:
            load_cols(WB, dr[f"wb{L}"], M_END, "WBm", "WB")
        else:
            load_group(L, 0, W_END if k == 1 else F_END)

    STOP = os.environ.get("KSTOP", "")
    load_consts()
    for L in range(nlayers):
        if STOP == "consts":
            break
        kind = KINDS[L]
        src, src_res = (x_in, "xin") if L == 0 else (scratch[(L - 1) % 2], f"xs{(L - 1) % 2}_")
        dst, dst_res = (y_out, "yout") if L == nlayers - 1 else (scratch[L % 2], f"xs{L % 2}_")
        load_wx(f"wg{L}")
        if L == 0:
            load_b_weights(L)
        if kind == 0:
            t.dma("sp", "gq", lambda e, L=L: e.dma_start(out=gq[:, :], in_=dr[f"gq{L}"].partition_broadcast(128)),
                  writes=["gq"])
            t.dma("sp", "X16", lambda e: [e.dma_start(out=X16[64:96, 0:2048], in_=dr["cosf"][:, :]),
                                          e.dma_start(out=X16[64:96, 2048:4096], in_=dr["sinf"][:, :])],
                  writes=X16_ALL, n=2)
        if STOP == "loads":
            break
        phase_A(L, src, src_res, kind)
        if STOP == "A":
            break
        load_wx(f"wo{L}")
        if kind == 0:
            mixer_mla(L)
        elif kind == 1:
            mixer_swa(L)
        else:
            mixer_fox(L)
        if STOP == "B":
            break
        t.barrier()
        if L + 1 < nlayers:
            load_b_weights(L + 1)
        phase_C(L, src, src_res, dst, dst_res)
        t.barrier()
    t.wait_all("sp")
    with nc.Block() as block:
        t.emit(block)
    return nc


_CACHE = {}


def _run(inputs, nlayers=DEPTH):
    inp = {k: np.asarray(v) for k, v in inputs.items()}
    consts = _host_consts()
    weights = _host_weights(inp)
    shared = {}
    shared.update(consts)
    shared.update(weights)
    shapes = {k: v.shape for k, v in shared.items()}
    if nlayers not in _CACHE:
        _CACHE[nlayers] = build_program(nlayers, shapes)
    nc = _CACHE[nlayers]
    x = np.ascontiguousarray(inp["x"].astype(np.float32))
    import os
    ncores = int(os.environ.get("KCORES", "8"))
    in_maps = []
    for b in range(ncores):
        m = {"x": x[b]}
        m.update(shared)
        in_maps.append(m)
    res = run_bass_kernel_spmd(nc, in_maps, core_ids=list(range(ncores)))
    return np.stack([np.asarray(r["y"]) for r in res.results], axis=0).astype(np.float32)


def kernel(**inputs):
    return _run(inputs, DEPTH)
```

```python
import os
import numpy as np
import concourse.bass as bass
import concourse.mybir as mybir
from concourse.bass_utils import run_bass_kernel_spmd

F32 = mybir.dt.float32
BF16 = mybir.dt.bfloat16
AF = mybir.ActivationFunctionType
ALU = mybir.AluOpType

S = 2048
D = 1024
NT = S // 128
NGRP = S // 512
DEPTH = 4
ALPHA = float((2 * DEPTH) ** 0.25)
LN_EPS = 1e-5
RMS_EPS = 1e-6
NEG = -30000.0


class Trk:
    ENG = ("pe", "act", "dve", "pool", "sp")
    CMP = ("pe", "act", "dve", "pool")

    def __init__(self, nc):
        self.nc = nc
        self.streams = {e: [] for e in self.ENG}
        self.cnt = {e: 0 for e in self.ENG}
        self.esem = {}
        self.dsem = {}
        self.dcnt = {}
        self.last_w = {}
        self.readers = {}
        self.seen = {e: {} for e in self.ENG}

    def sem(self, key):
        if key in self.ENG:
            if key not in self.esem:
                self.esem[key] = self.nc.alloc_semaphore("cs_" + key)
            return self.esem[key]
        if key not in self.dsem:
            self.dsem[key] = self.nc.alloc_semaphore("ds_" + key)
            self.dcnt[key] = 0
        return self.dsem[key]

    def _need(self, e, ev, waits):
        if ev is None:
            return
        k, v = ev
        if k == e and e == "pe":
            return
        if self.seen[e].get(k, 0) >= v:
            return
        if waits.get(k, 0) < v:
            waits[k] = v

    def _deps(self, e, reads, writes):
        waits = {}
        for r in reads:
            self._need(e, self.last_w.get(r), waits)
        for w in writes:
            self._need(e, self.last_w.get(w), waits)
            for k, v in self.readers.get(w, {}).items():
                self._need(e, (k, v), waits)
        for k, v in waits.items():
            self.seen[e][k] = v
            self.streams[e].append(("wait", k, v))

    def _commit(self, ev, reads, writes):
        for r in reads:
            d = self.readers.setdefault(r, {})
            if d.get(ev[0], 0) < ev[1]:
                d[ev[0]] = ev[1]
        for w in writes:
            self.last_w[w] = ev
            self.readers[w] = {}

    def op(self, e, fn, reads=(), writes=()):
        self._deps(e, reads, writes)
        self.sem(e)
        self.cnt[e] += 1
        ev = (e, self.cnt[e])
        self.streams[e].append(("op", fn, e))
        self._commit(ev, reads, writes)
        return ev

    def dma(self, q, slot, fn, reads=(), writes=(), n=1):
        self._deps(q, reads, writes)
        self.sem(slot)
        self.dcnt[slot] += 16 * n
        ev = (slot, self.dcnt[slot])
        self.streams[q].append(("dma", fn, slot))
        self._commit(ev, reads, writes)
        return ev

    def barrier(self):
        for e in self.ENG:
            for k in self.CMP:
                if k != e and k in self.esem and self.cnt[k] > self.seen[e].get(k, 0):
                    self.seen[e][k] = self.cnt[k]
                    self.streams[e].append(("wait", k, self.cnt[k]))

    def wait_all(self, e):
        for k in list(self.esem):
            if k != e and self.cnt[k] > self.seen[e].get(k, 0):
                self.seen[e][k] = self.cnt[k]
                self.streams[e].append(("wait", k, self.cnt[k]))
        for k in self.dsem:
            if self.dcnt[k] > self.seen[e].get(k, 0):
                self.seen[e][k] = self.dcnt[k]
                self.streams[e].append(("wait", k, self.dcnt[k]))

    def emit(self, block):
        tr = self

        def run(eng, e):
            for it in tr.streams[e]:
                if it[0] == "wait":
                    eng.wait_ge(tr.sem(it[1]), it[2])
                elif it[0] == "op":
                    ins = it[1](eng)
                    ins.then_inc(tr.esem[e], 1)
                else:
                    inss = it[1](eng)
                    if not isinstance(inss, (list, tuple)):
                        inss = [inss]
                    for i in inss:
                        i.then_inc(tr.dsem[it[2]], 16)

        @block.tensor
        def _(eng):
            run(eng, "pe")

        @block.scalar
        def _(eng):
            run(eng, "act")

        @block.vector
        def _(eng):
            run(eng, "dve")

        @block.gpsimd
        def _(eng):
            run(eng, "pool")

        @block.sync
        def _(eng):
            run(eng, "sp")


class Ring:
    def __init__(self, items):
        self.items = items
        self.i = 0

    def next(self):
        it = self.items[self.i % len(self.items)]
        self.i += 1
        return it


def _chunk(w):
    k, n = w.shape
    return np.ascontiguousarray(w.reshape(k // 128, 128, n).transpose(1, 0, 2).reshape(128, -1))


KINDS = [0, 1, 2, 0]
JIDX = [0, 0, 0, 1]

M_WL = 0
M_WQ = M_WL + 8 * 672
M_WQS = M_WQ + 3 * 1536
M_WKK = M_WQS + 3 * 576
M_WKV = M_WKK + 2 * 1024
M_END = M_WKV + 2 * 1024
F_WQ = 0
F_WK = F_WQ + 8 * 256
F_WV = F_WK + 8 * 256
F_END = F_WV + 8 * 256
W_WQ = 0
W_WK = W_WQ + 8 * 256
W_WV = W_WK + 8 * 64
W_END = W_WV + 8 * 64
SLOT = 6400


def _host_consts():
    c = {}
    c["ident"] = np.eye(128, dtype=np.float32)
    k = np.arange(128)[:, None]
    q = np.arange(128)[None, :]
    c["cmask"] = np.where(k <= q, 0.0, NEG).astype(np.float32)
    c["u32"] = (k <= q).astype(np.float32)
    c["ones32"] = np.ones((128, 128), np.float32)
    prev = np.where(k > q, -8.0 * (q + 128 - k), -1e9).astype(np.float32)
    cur = np.where(k <= q, -8.0 * (q - k), -1e9).astype(np.float32)
    dm = np.concatenate([prev, cur, prev, cur], axis=1)
    dmf = np.concatenate([np.full((128, 128), -1e9, np.float32), cur, prev, cur], axis=1)
    c["dm4"] = np.ascontiguousarray(dm)
    c["dm4f"] = np.ascontiguousarray(dmf)
    slopes = (2.0 ** (-8.0 * np.arange(1, 17, dtype=np.float32) / 16)).astype(np.float32)
    c["slopes"] = slopes
    inv = (10000.0 ** (-np.arange(0, 32, 2, dtype=np.float32) / 32)).astype(np.float32)
    ang = np.arange(S, dtype=np.float32)[:, None] * inv[None, :]
    cos = np.cos(ang).astype(np.float32)
    sin = np.sin(ang).astype(np.float32)
    cc = np.concatenate([cos, cos], axis=1)
    ss = np.concatenate([-sin, sin], axis=1)
    c["cc"] = np.ascontiguousarray(cc.reshape(NT, 128, 32).transpose(1, 0, 2).reshape(128, -1))
    c["ss"] = np.ascontiguousarray(ss.reshape(NT, 128, 32).transpose(1, 0, 2).reshape(128, -1))
    c["cosf"] = np.ascontiguousarray(cc.T)
    c["sinf"] = np.ascontiguousarray(ss.T)
    return c


def _host_weights(inp):
    w = {}
    for L in range(DEPTH):
        kind, j = KINDS[L], JIDX[L]
        if kind == 0:
            w_in = inp["mla_w_in"][j]
            wl = np.concatenate([w_in[:, 0:384], w_in[:, 640:672], w_in[:, 384:640]], axis=1)
            wq = inp["mla_w_q_up"][j]
            wqs = np.zeros((384, 576), np.float32)
            for h in range(16):
                base = h * 96 + 64
                wqs[:, 64 + h * 32: 64 + h * 32 + 16] = wq[:, base + 16: base + 32]
                wqs[:, 64 + h * 32 + 16: 64 + h * 32 + 32] = wq[:, base: base + 16]
            wkv = inp["mla_w_kv_up"][j].reshape(256, 16, 128)
            wkk = wkv[:, :, 0:64].reshape(256, 1024)
            wkvv = wkv[:, :, 64:128].reshape(256, 1024)
            arena = np.concatenate([_chunk(wl), _chunk(wq), _chunk(wqs), _chunk(wkk), _chunk(wkvv)], axis=1)
            assert arena.shape[1] == M_END
            w[f"wb{L}"] = np.ascontiguousarray(arena)
            w[f"wg{L}"] = _chunk(w_in[:, 672:1696])
            w[f"wo{L}"] = _chunk(inp["mla_w_out"][j])
            w[f"gq{L}"] = np.concatenate([inp["mla_q_norm"][j], inp["mla_kv_norm"][j]]).astype(np.float32)
        elif kind == 1:
            w_in = inp["swa_w_in"][j]
            slots = []
            for g in range(4):
                wq = w_in[:, g * 256:(g + 1) * 256]
                wk = w_in[:, 1024 + g * 64: 1024 + (g + 1) * 64]
                wv = w_in[:, 1280 + g * 64: 1280 + (g + 1) * 64]
                slots.append(np.concatenate([_chunk(wq), _chunk(wk), _chunk(wv)], axis=1))
            w[f"wb{L}"] = np.ascontiguousarray(np.stack(slots, 0))
            w[f"wg{L}"] = _chunk(w_in[:, 1536:2560])
            w[f"wo{L}"] = _chunk(inp["swa_w_out"][j])
            w[f"sinks{L}"] = inp["swa_sinks"][j].astype(np.float32)
        else:
            w_in = inp["fox_w_in"][j]
            slots = []
            for g in range(4):
                wq = w_in[:, g * 256:(g + 1) * 256]
                wk = w_in[:, 1024 + g * 256: 1024 + (g + 1) * 256]
                wv = w_in[:, 2048 + g * 256: 2048 + (g + 1) * 256]
                slots.append(np.concatenate([_chunk(wq), _chunk(wk), _chunk(wv)], axis=1))
            w[f"wb{L}"] = np.ascontiguousarray(np.stack(slots, 0))
            w[f"wf{L}"] = _chunk(w_in[:, 3072:3088])
            w[f"wg{L}"] = _chunk(w_in[:, 3088:4112])
            w[f"wo{L}"] = _chunk(inp["fox_w_out"][j])
            w[f"bf{L}"] = inp["fox_b_f"][j].astype(np.float32)
    w["lng"] = np.ascontiguousarray(inp["ln_g"].astype(np.float32))
    w["lnb"] = np.ascontiguousarray(inp["ln_b"].astype(np.float32))
    return w


def build_program(nlayers=DEPTH, shapes=None):
    nc = bass.Bass("TRN2", target_bir_lowering=False)
    t = Trk(nc)
    dr = {}

    def din(name, shape):
        dr[name] = nc.dram_tensor(name, list(shape), F32, kind="ExternalInput").ap()
        return dr[name]

    x_in = din("x", [S, D])
    for name, shp in shapes.items():
        din(name, shp)
    y_out = nc.dram_tensor("y", [S, D], F32, kind="ExternalOutput").ap()
    scratch = [nc.dram_tensor(f"xs{i}", [S, D], F32).ap() for i in range(2)]

    def sb(name, shape, dt):
        return nc.alloc_sbuf_tensor("sb_" + name, list(shape), dt)

    R32 = sb("R32", [128, 16384], BF16)
    Q32 = sb("Q32", [128, 16384], BF16)
    GB = sb("GB", [128, NT, 1024], BF16)
    VA = sb("VA", [128, NT, 4, 65], BF16)
    X16 = sb("X16", [128, 4096], F32)
    WX = sb("WX", [128, 8192], BF16)
    WB = sb("WB", [128, M_END], BF16)
    xs_t = [sb(f"xs_sb{i}", [128, 1024], F32) for i in range(2)]
    pt_t = [sb(f"pt{i}", [128, 512], BF16) for i in range(3)]
    ln_t = [sb(f"lnt{i}", [128, 768], BF16) for i in range(2)]
    og_t = [sb(f"ogt{i}", [128, 1024], BF16) for i in range(3)]
    ident = sb("ident", [128, 128], BF16)
    cmask = sb("cmask", [128, 128], BF16)
    u32 = sb("u32", [128, 128], F32)
    ones32 = sb("ones32", [128, 128], F32)
    dm4 = X16[:, 0:512]
    dm4f = X16[:, 512:1024]
    swring = Ring([(X16[:, 1024 + i * 512:1536 + i * 512], f"swt{i}") for i in range(3)])
    gq = sb("gq", [128, 640], F32)
    cc = sb("cc", [128, NT, 32], F32)
    ssg = sb("ssg", [128, NT, 32], F32)
    small = sb("small", [128, 256], F32)
    tmp_t = [sb(f"tmpn{i}", [128, 4, 64], F32) for i in range(2)]
    junk = sb("junk", [128, 384], BF16)
    rk = sb("rk", [128, 128], F32)
    qtm_t = [sb(f"qtm{i}", [128, 260], BF16) for i in range(2)]
    ktm_t = [sb(f"ktm{i}", [128, 256], BF16) for i in range(2)]
    qtring = Ring([(qtm_t[i], f"qtm{i}") for i in range(2)])
    ktring = Ring([(ktm_t[i], f"ktm{i}") for i in range(2)])

    ms = small[:, 0:2]
    lnms = small[:, 2:4]
    rstd2 = small[:, 4:6]
    lnr = small[:, 6:8]
    rc_t = [small[:, 8:12], small[:, 12:16]]
    slopes_bc = small[:, 16:32]
    sinks_bc = small[:, 32:48]
    esink = small[:, 48:64]
    bf_bc = small[:, 64:80]
    zt = small[:, 80:96]
    et = small[:, 96:112]
    carry = [small[:, 112:128], small[:, 128:144]]
    stats = small[:, 144:156]
    mv = small[:, 156:158]
    den_t = small[:, 160:164]


    PS = [nc.alloc_psum_tensor(f"ps{i}", [128, 512], F32) for i in range(6)]
    PT_ = [nc.alloc_psum_tensor(f"pst{i}", [128, 1024], BF16) for i in range(2)]
    short = Ring([(PS[i], f"ps{i}") for i in range(4)])
    longr = Ring([(PS[4], "ps4"), (PS[5], "ps5")])
    proj6 = Ring([(PS[i], f"ps{i}") for i in range(6)])
    tring = Ring([(PT_[0], "pst0"), (PT_[1], "pst1")])
    ptring = Ring([(pt_t[i], f"pt{i}") for i in range(3)])
    xsring = Ring([(xs_t[i], f"xs_sb{i}") for i in range(2)])
    lnring = Ring([(ln_t[i], f"lnt{i}") for i in range(2)])
    ogring = Ring([(og_t[i], f"ogt{i}") for i in range(3)])
    rcring = Ring([(rc_t[i], f"rc{i}") for i in range(2)])
    tmpring = Ring([(tmp_t[i], f"tmpn{i}") for i in range(2)])
    evtog = [0]

    def evac_engine():
        evtog[0] += 1
        return "dve" if evtog[0] % 3 else "act"

    def copy_op(e, out, in_, reads, writes):
        if e == "act":
            t.op("act", lambda eng: eng.activation(out=out, in_=in_, func=AF.Copy), reads, writes)
        else:
            t.op(e, lambda eng: eng.tensor_copy(out=out, in_=in_), reads, writes)

    xT = R32[:, :].rearrange("p (c n) -> p c n", c=8)
    xT_res = [f"xT{i}" for i in range(NT)]
    X16_ALL = ["X16", "Zg", "xo", "lng", "lnb", "z0", "z1", "cum8", "Wf", "swt0", "swt1", "swt2", "dm4", "dm4f", "u32x", "carry0"]

    def xT_reads(G):
        return xT_res[4 * G:4 * G + 4]

    def load_consts():
        t.dma("pool", "c_id", lambda e: [e.dma_start(out=ident[:, :], in_=dr["ident"][:, :]),
                                         e.dma_start(out=cmask[:, :], in_=dr["cmask"][:, :])],
              writes=["ident", "cmask"], n=2)
        t.dma("sp", "c_f32", lambda e: [e.dma_start(out=u32[:, :], in_=dr["u32"][:, :]),
                                        e.dma_start(out=ones32[:, :], in_=dr["ones32"][:, :]),
                                        e.dma_start(out=cc[:, :, :].rearrange("p a b -> p (a b)"), in_=dr["cc"][:, :]),
                                        e.dma_start(out=ssg[:, :, :].rearrange("p a b -> p (a b)"), in_=dr["ss"][:, :]),
                                        e.dma_start(out=slopes_bc, in_=dr["slopes"].partition_broadcast(128))],
              writes=["u32", "ones32", "cc", "ssg", "slopes"], n=5)
        t.op("pool", lambda e: e.memset(VA[:, :, :, 64:65], 1.0), writes=["VA"])
        for i in range(2):
            t.op("pool", lambda e, i=i: e.memset(ln_t[i][:, 640:768], 0.0), writes=[f"lnt{i}"])

    def load_cols(dst, src, ncols, slot, res):
        step = 2048
        pieces = [(c0, min(step, ncols - c0)) for c0 in range(0, ncols, step)]

        def fn(e):
            return [e.dma_start(out=dst[:, c0:c0 + n], in_=src[:, c0:c0 + n]) for c0, n in pieces]
        t.dma("pool", slot, fn, writes=[res], n=len(pieces))

    def load_wx(name):
        load_cols(WX, dr[name], 8192, "WX", "WX")

    def phase_A(L, src, src_res, kind, wg_in_q32=False):
        if wg_in_q32:
            Wg = Q32[:, 1024:9216].rearrange("p (c n) -> p c n", c=8)
            wgres = "WGQ"
        else:
            Wg = WX[:, :].rearrange("p (c n) -> p c n", c=8)
            wgres = "WX"
        if kind == 0:
            Wl = WB[:, M_WL:M_WQ].rearrange("p (c n) -> p c n", c=8)
            LT = Q32[:, 0:10240].rearrange("p (c n) -> p c n", c=5)
            KR = Q32[:, 10240:12288]
        xbring = Ring([(Q32[:, 12288 + i * 1024:12288 + (i + 1) * 1024], f"xb{i}") for i in range(2)])
        ebring = Ring([(Q32[:, 14336 + i * 1024:14336 + (i + 1) * 1024].bitcast(F32), f"eb{i}") for i in range(2)])
        def front(ti):
            xs, xs_r = xsring.next()
            t.dma("sp", xs_r, lambda e, xs=xs, ti=ti: e.dma_start(out=xs[:, :], in_=src[ti * 128:(ti + 1) * 128, :]),
                  reads=[f"{src_res}{ti}"], writes=[xs_r])
            xb_, xb_r = xbring.next()
            t.op("dve", lambda e, xs=xs, xb_=xb_: e.tensor_copy(out=xb_, in_=xs[:, :]),
                 reads=[xs_r], writes=[xb_r])
            tb, tb_r = tring.next()

            def tr8(e, tb=tb, xb_=xb_):
                for k in range(8):
                    ins = e.transpose(out=tb[:, k * 128:(k + 1) * 128], in_=xb_[:, k * 128:(k + 1) * 128],
                                      identity=ident[:, :])
                return ins
            t.op("pe", tr8, reads=[xb_r, "ident"], writes=[tb_r])
            t.op("dve", lambda e, tb=tb, ti=ti: e.tensor_copy(
                out=xT[:, :, ti * 128:(ti + 1) * 128],
                in_=tb[:, :].rearrange("p (c n) -> p c n", c=8)), reads=[tb_r],
                writes=[xT_res[ti], "xoR0h0", "xoR0h1", "xoR1h0", "xoR1h1"])

        def main(ti):
            KSUB = int(os.environ.get("KSUB", "9"))
            if KSUB < 2:
                return
            for half in range(2):
                ps, ps_r = short.next()

                def gmm(e, ps=ps, ti=ti, half=half):
                    for k in range(8):
                        ins = e.matmul(ps[:, :], lhsT=xT[:, k, ti * 128:(ti + 1) * 128],
                                       rhs=Wg[:, k, half * 512:(half + 1) * 512], start=(k == 0), stop=(k == 7))
                    return ins
                t.op("pe", gmm, reads=[xT_res[ti], wgres], writes=[ps_r])
                gres = [f"G{ti}_{k}" for k in range(half * 4, half * 4 + 4)]
                if True:
                    t.op("act", lambda e, ps=ps, ti=ti, half=half: e.activation(
                        out=GB[:, ti, half * 512:(half + 1) * 512], in_=ps[:, :], func=AF.Silu),
                        reads=[ps_r], writes=gres)
                else:
                    eb, eb_r = ebring.next()
                    t.op("act", lambda e, ps=ps, eb=eb: e.activation(out=eb, in_=ps[:, :], func=AF.Exp, scale=-1.0),
                         reads=[ps_r], writes=[eb_r])
                    t.op("pool", lambda e, eb=eb: e.tensor_scalar(out=eb, in0=eb, scalar1=1.0, scalar2=None,
                                                                  op0=ALU.add), reads=[eb_r], writes=[eb_r])
                    t.op("dve", lambda e, eb=eb: e.reciprocal(out=eb, in_=eb), reads=[eb_r], writes=[eb_r])
                    t.op("dve", lambda e, ps=ps, eb=eb, ti=ti, half=half: e.tensor_tensor(
                        out=GB[:, ti, half * 512:(half + 1) * 512], in0=ps[:, :], in1=eb, op=ALU.mult),
                        reads=[ps_r, eb_r], writes=gres)
            return

        def lat(ti):
            KSUB = 9
            psA, psA_r = short.next()
            psB, psB_r = short.next()

            def lmm(e, psA=psA, psB=psB, ti=ti):
                for k in range(8):
                    e.matmul(psA[:, 0:416], lhsT=xT[:, k, ti * 128:(ti + 1) * 128], rhs=Wl[:, k, 0:416],
                             start=(k == 0), stop=(k == 7))
                for k in range(8):
                    ins = e.matmul(psB[:, 0:256], lhsT=xT[:, k, ti * 128:(ti + 1) * 128], rhs=Wl[:, k, 416:672],
                                   start=(k == 0), stop=(k == 7))
                return ins
            t.op("pe", lmm, reads=[xT_res[ti], "WB"], writes=[psA_r, psB_r])
            pr = ti % 2
            ms_ = small[:, 0:2] if pr == 0 else small[:, 182:184]
            lnms_ = small[:, 2:4] if pr == 0 else small[:, 184:186]
            rstd_ = small[:, 4:6] if pr == 0 else small[:, 186:188]
            rk_ = rk[:, pr * 64:(pr + 1) * 64]
            t.op("act", lambda e, psA=psA, ms_=ms_: e.activation(out=junk[:, 0:384], in_=psA[:, 0:384], func=AF.Square,
                                                                   scale=float(384 ** -0.5), accum_out=ms_[:, 0:1]),
                 reads=[psA_r], writes=[f"ms0_{pr}", "junk"])
            t.op("act", lambda e, psB=psB, ms_=ms_: e.activation(out=junk[:, 0:256], in_=psB[:, 0:256], func=AF.Square,
                                                                   scale=float(256 ** -0.5), accum_out=ms_[:, 1:2]),
                 reads=[psB_r], writes=[f"ms1_{pr}", "junk"])
            if KSUB < 4:
                return
            t.op("act", lambda e, ms_=ms_, lnms_=lnms_: e.activation(out=lnms_, in_=ms_, func=AF.Ln, bias=RMS_EPS,
                                                                       scale=1.0),
                 reads=[f"ms0_{pr}", f"ms1_{pr}"], writes=[f"lnms{pr}"])
            t.op("act", lambda e, lnms_=lnms_, rstd_=rstd_: e.activation(out=rstd_, in_=lnms_, func=AF.Exp, scale=-0.5),
                 reads=[f"lnms{pr}"], writes=[f"rstd{pr}"])
            if KSUB < 5:
                return
            lt, lt_r = lnring.next()
            t.op("dve", lambda e, psA=psA, lt=lt, rstd_=rstd_: e.scalar_tensor_tensor(
                out=lt[:, 0:384], in0=psA[:, 0:384], scalar=rstd_[:, 0:1], in1=gq[:, 0:384],
                op0=ALU.mult, op1=ALU.mult), reads=[psA_r, f"rstd{pr}", "gq"], writes=[lt_r + "a"])
            t.op("dve", lambda e, psB=psB, lt=lt, rstd_=rstd_: e.scalar_tensor_tensor(
                out=lt[:, 384:640], in0=psB[:, 0:256], scalar=rstd_[:, 1:2], in1=gq[:, 384:640],
                op0=ALU.mult, op1=ALU.mult), reads=[psB_r, f"rstd{pr}", "gq"], writes=[lt_r + "b"])
            t.op("dve", lambda e, psA=psA, ti=ti, rk_=rk_: e.tensor_tensor(out=rk_[:, 0:32], in0=psA[:, 384:416],
                                                                            in1=cc[:, ti, :], op=ALU.mult),
                 reads=[psA_r, "cc"], writes=[f"rk1_{pr}"])
            t.op("dve", lambda e, psA=psA, ti=ti, rk_=rk_: e.tensor_tensor(out=rk_[:, 32:48], in0=psA[:, 400:416],
                                                                            in1=ssg[:, ti, 0:16], op=ALU.mult),
                 reads=[psA_r, "ssg"], writes=[f"rk2a_{pr}"])
            t.op("dve", lambda e, psA=psA, ti=ti, rk_=rk_: e.tensor_tensor(out=rk_[:, 48:64], in0=psA[:, 384:400],
                                                                            in1=ssg[:, ti, 16:32], op=ALU.mult),
                 reads=[psA_r, "ssg"], writes=[f"rk2b_{pr}"])
            t.op("pool", lambda e, lt=lt, rk_=rk_: e.tensor_tensor(out=lt[:, 704:736], in0=rk_[:, 0:32],
                                                                  in1=rk_[:, 32:64], op=ALU.add),
                 reads=[f"rk1_{pr}", f"rk2a_{pr}", f"rk2b_{pr}"], writes=[lt_r + "c"])
            return lt, lt_r

        def tail(ti, lt, lt_r):
            tb, tb_r = tring.next()

            def tr6(e, tb=tb, lt=lt):
                for c in range(6):
                    ins = e.transpose(out=tb[:, c * 128:(c + 1) * 128], in_=lt[:, c * 128:(c + 1) * 128],
                                      identity=ident[:, :])
                return ins
            t.op("pe", tr6, reads=[lt_r, lt_r + "a", lt_r + "b", lt_r + "c", "ident"],
                 writes=[tb_r])
            t.op("dve", lambda e, tb=tb, ti=ti: e.tensor_copy(
                out=LT[:, :, ti * 128:(ti + 1) * 128], in_=tb[:, 0:640].rearrange("p (c n) -> p c n", c=5)),
                reads=[tb_r], writes=[f"LT{ti}"])
            t.op("dve", lambda e, tb=tb, ti=ti: e.tensor_copy(
                out=KR[64:96, ti * 128:(ti + 1) * 128], in_=tb[64:96, 640:768]),
                reads=[tb_r], writes=["KR"])

        front(0)
        front(1)
        pend = None
        for ti in range(NT):
            main(ti)
            if ti + 2 < NT:
                front(ti + 2)
            if kind == 0:
                r = lat(ti)
                if pend is not None:
                    tail(*pend)
                pend = (ti,) + tuple(r)
        if pend is not None:
            tail(*pend)

    def attn_out(h, G, po, po_r, extra_den=None):
        rc, rc_r = rcring.next()
        tmp, tmp_r = tmpring.next()
        pov = po[:, 0:260].rearrange("p (i c) -> p i c", c=65)
        if extra_den is None:
            t.op("dve", lambda e: e.reciprocal(out=rc, in_=pov[:, :, 64]), reads=[po_r], writes=[rc_r])
        else:
            t.op("dve", lambda e: e.tensor_scalar(out=den_t, in0=pov[:, :, 64], scalar1=extra_den, scalar2=None,
                                                  op0=ALU.add), reads=[po_r, "esink"], writes=["den"])
            t.op("dve", lambda e: e.reciprocal(out=rc, in_=den_t), reads=["den"], writes=[rc_r])
        t.op("dve", lambda e: e.tensor_tensor(out=tmp[:, :, :], in0=pov[:, :, 0:64],
                                              in1=rc.unsqueeze(2).broadcast_to([128, 4, 64]), op=ALU.mult),
             reads=[po_r, rc_r], writes=[tmp_r])
        gv = GB[:, 4 * G:4 * G + 4, h * 64:(h + 1) * 64]
        gres = [f"G{ti}_{h // 2}" for ti in range(4 * G, 4 * G + 4)]
        t.op("pool", lambda e: e.tensor_tensor(out=gv, in0=tmp[:, :, :], in1=gv, op=ALU.mult),
             reads=[tmp_r] + gres, writes=gres)

    LA = 2
    pvq = []

    def push(fns):
        pvq.append(fns)
        if len(pvq) > LA:
            for f in pvq.pop(0):
                f()

    def flush():
        while pvq:
            for f in pvq.pop(0):
                f()

    def attn_dense(h, hl, khl, kd, scale, QT, KT, bias_fn, kextra):
        qres, kres = f"QT{hl}", f"KT{khl}"
        for G in range(NGRP):
            po, po_r = longr.next()
            first = [True]
            prev = None

            def pv(prev, po=po, po_r=po_r, first=first):
                j, pt, pt_r, r = prev
                items = []
                for i in range(max(r, 0), 4):
                    items.append((i, first[0]))
                    first[0] = False

                def fn(e):
                    ins = None
                    for i, st in items:
                        ins = e.matmul(po[:, i * 65:(i + 1) * 65], lhsT=pt[:, i * 128:(i + 1) * 128],
                                       rhs=VA[:, j, khl, :], start=st, stop=False, skip_group_check=True)
                    return ins
                t.op("pe", fn, reads=[pt_r, "VA"], writes=[po_r])

            for j in range(4 * G + 4):
                ps, ps_r = short.next()
                pt, pt_r = ptring.next()
                r = j - 4 * G
                c0 = max(r, 0) * 128
                kt = KT[0:kd, khl, j * 128:(j + 1) * 128]

                def qk(e, ps=ps, r=r, c0=c0, kt=kt, G=G):
                    if r < 0:
                        return e.matmul(ps[:, :], lhsT=kt, rhs=QT[0:kd, hl, G * 512:(G + 1) * 512],
                                        start=True, stop=True)
                    e.matmul(ps[:, c0:c0 + 128], lhsT=kt, rhs=QT[0:kd, hl, G * 512 + c0:G * 512 + c0 + 128],
                             start=True, stop=False, skip_group_check=True)
                    ins = e.matmul(ps[:, c0:c0 + 128], lhsT=ident[:, :], rhs=cmask[:, :], start=False, stop=True,
                                   skip_group_check=True)
                    if c0 + 128 < 512:
                        ins = e.matmul(ps[:, c0 + 128:512], lhsT=kt,
                                       rhs=QT[0:kd, hl, G * 512 + c0 + 128:(G + 1) * 512],
                                       start=False, stop=True, skip_group_check=True)
                    return ins
                t.op("pe", qk, reads=[qres, kres, "ident", "cmask"] + kextra, writes=[ps_r])
                b_ap, b_res = bias_fn(j)
                t.op("act", lambda e, ps=ps, pt=pt, c0=c0, b_ap=b_ap: e.activation(
                    out=pt[:, c0:512], in_=ps[:, c0:512], func=AF.Exp, scale=scale, bias=b_ap),
                    reads=[ps_r] + b_res, writes=[pt_r])
                fns = [lambda pv=pv, a=(j, pt, pt_r, r): pv(a)]
                if j == 4 * G + 3:
                    fns.append(lambda h=h, G=G, po=po, po_r=po_r: attn_out(h, G, po, po_r))
                push(fns)

    def attn_band(h, hl, QT, KT):
        qres, kres = f"QT{hl}", "KT0"
        for G in range(NGRP):
            po, po_r = longr.next()
            first = [True]
            for pp in range(2):
                i0 = 4 * G + 2 * pp
                ps, ps_r = short.next()
                pt, pt_r = ptring.next()

                def qk(e, ps=ps, i0=i0):
                    ins = None
                    st = True
                    for tt in range(2):
                        i = i0 + tt
                        jprev = max(i - 1, 0)
                        for jj, j in enumerate((jprev, i)):
                            col = (tt * 2 + jj) * 128
                            ins = e.matmul(ps[:, col:col + 128], lhsT=KT[0:64, 0, j * 128:(j + 1) * 128],
                                           rhs=QT[0:64, hl, i * 128:(i + 1) * 128], start=st, stop=False,
                                           skip_group_check=True)
                            st = False
                    return ins
                t.op("pe", qk, reads=[qres, kres], writes=[ps_r])
                dm = dm4f if i0 == 0 else dm4
                swt, swt_r = swring.next()
                t.op("dve", lambda e, ps=ps, dm=dm, swt=swt: e.scalar_tensor_tensor(
                    out=swt, in0=dm, scalar=slopes_bc[:, h:h + 1], in1=ps[:, :],
                    op0=ALU.mult, op1=ALU.add), reads=[ps_r, "dm4", "dm4f", "slopes"], writes=[swt_r])
                t.op("act", lambda e, pt=pt, swt=swt: e.activation(out=pt[:, :], in_=swt, func=AF.Exp, scale=0.125),
                     reads=[swt_r], writes=[pt_r])

                items = []
                for tt in range(2):
                    i = i0 + tt
                    oc = (pp * 2 + tt) * 65
                    for jj, j in enumerate((i - 1, i)):
                        if j < 0:
                            continue
                        items.append((oc, (tt * 2 + jj) * 128, j, first[0]))
                        first[0] = False

                def pv(e, pt=pt, items=items, po=po):
                    ins = None
                    for oc, col, j, st in items:
                        ins = e.matmul(po[:, oc:oc + 65], lhsT=pt[:, col:col + 128], rhs=VA[:, j, 0, :],
                                       start=st, stop=False, skip_group_check=True)
                    return ins
                fns = [lambda pv=pv, pt_r=pt_r, po_r=po_r: t.op("pe", pv, reads=[pt_r, "VA"], writes=[po_r])]
                if pp == 1:
                    fns.append(lambda h=h, G=G, po=po, po_r=po_r: attn_out(h, G, po, po_r,
                                                                          extra_den=esink[:, h:h + 1]))
                push(fns)

    def proj_fm(dst, rows, chunks, lhs_fn, rhs_fn, reads, writes, extra=None, eng=None):
        ps, ps_r = proj6.next()

        def fn(e):
            ins = None
            for ci, c in enumerate(chunks):
                ins = e.matmul(ps[0:rows, :], lhsT=lhs_fn(c), rhs=rhs_fn(c), start=(ci == 0),
                               stop=(ci == len(chunks) - 1))
            if extra is not None:
                ins = extra(e, ps)
            return ins
        t.op("pe", fn, reads=reads, writes=[ps_r])
        copy_op(eng or evac_engine(), dst, ps[0:rows, :], [ps_r], writes)

    def mixer_mla(L):
        t.barrier()
        QT = R32[:, 0:8192].rearrange("p (h n) -> p h n", h=4)
        KT = R32[:, 8192:16384].rearrange("p (h n) -> p h n", h=4)
        LT = Q32[:, 0:10240].rearrange("p (c n) -> p c n", c=5)
        KR = Q32[:, 10240:12288]
        T1 = [Q32[:, 12288 + i * 1024:12288 + (i + 1) * 1024].bitcast(F32) for i in range(2)]
        T2 = [Q32[:, 14336 + i * 1024:14336 + (i + 1) * 1024].bitcast(F32) for i in range(2)]
        t1r = Ring([(T1[i], f"T1_{i}") for i in range(2)])
        t2r = Ring([(T2[i], f"T2_{i}") for i in range(2)])
        Wq = WB[:, M_WQ:M_WQS].rearrange("p (c n) -> p c n", c=3)
        Wqs = WB[:, M_WQS:M_WKK].rearrange("p (c n) -> p c n", c=3)
        Wkk = WB[:, M_WKK:M_WKV].rearrange("p (c n) -> p c n", c=2)
        Wkv = WB[:, M_WKV:M_END].rearrange("p (c n) -> p c n", c=2)
        cosF = X16[:, 0:2048]
        sinF = X16[:, 2048:4096]
        ltall = [f"LT{i}" for i in range(NT)]
        t.op("pool", lambda e: e.tensor_copy(out=KT[64:96, :, :],
                                             in_=KR[64:96, :].unsqueeze(1).broadcast_to([32, 4, S])),
             reads=["KR"], writes=["KTrope"])
        scale = float(96 ** -0.5)
        for g in range(4):
            for hl in range(4):
                h = 4 * g + hl
                for G in range(NGRP):
                    cs = slice(G * 512, (G + 1) * 512)
                    lt_reads = ltall[4 * G:4 * G + 4]
                    proj_fm(KT[0:64, hl, cs], 64, [0, 1],
                            lambda c, h=h: Wkk[:, c, h * 64:(h + 1) * 64],
                            lambda c, cs=cs: LT[:, 3 + c, cs], lt_reads + ["WB"], [f"KT{hl}"], eng="act")
                    ps1, ps1_r = proj6.next()
                    ps2, ps2_r = proj6.next()

                    def qmm(e, ps1=ps1, ps2=ps2, h=h, cs=cs):
                        for c in range(3):
                            e.matmul(ps1[0:96, :], lhsT=Wq[:, c, h * 96:(h + 1) * 96], rhs=LT[:, c, cs],
                                     start=(c == 0), stop=(c == 2))
                        for c in range(3):
                            ins = e.matmul(ps2[0:96, :], lhsT=Wqs[:, c, h * 32:h * 32 + 96], rhs=LT[:, c, cs],
                                           start=(c == 0), stop=(c == 2))
                        return ins
                    t.op("pe", qmm, reads=lt_reads + ["WB"], writes=[ps1_r, ps2_r])
                    t.op("act", lambda e, ps1=ps1, hl=hl, cs=cs: e.activation(
                        out=QT[0:64, hl, cs], in_=ps1[0:64, :], func=AF.Copy), reads=[ps1_r], writes=[f"QT{hl}"])
                    a1, a1_r = t1r.next()
                    a2, a2_r = t2r.next()
                    t.op("dve", lambda e, ps1=ps1, a1=a1, cs=cs: e.tensor_tensor(
                        out=a1[64:96, :], in0=ps1[64:96, :], in1=cosF[64:96, cs], op=ALU.mult),
                        reads=[ps1_r, "X16"], writes=[a1_r])
                    t.op("dve", lambda e, ps2=ps2, a2=a2, cs=cs: e.tensor_tensor(
                        out=a2[64:96, :], in0=ps2[64:96, :], in1=sinF[64:96, cs], op=ALU.mult),
                        reads=[ps2_r, "X16"], writes=[a2_r])
                    t.op("pool", lambda e, a1=a1, a2=a2, hl=hl, cs=cs: e.tensor_tensor(
                        out=QT[64:96, hl, cs], in0=a1[64:96, :], in1=a2[64:96, :], op=ALU.add),
                        reads=[a1_r, a2_r], writes=[f"QT{hl}"])
            for ti in range(NT):
                ps, ps_r = short.next()

                def vmm(e, ps=ps, ti=ti, g=g):
                    for c in range(2):
                        ins = e.matmul(ps[:, 0:256], lhsT=LT[:, 3 + c, ti * 128:(ti + 1) * 128],
                                       rhs=Wkv[:, c, g * 256:(g + 1) * 256], start=(c == 0), stop=(c == 1))
                    return ins
                t.op("pe", vmm, reads=[f"LT{ti}", "WB"], writes=[ps_r])
                copy_op("dve", VA[:, ti, :, 0:64], ps[:, 0:256].rearrange("p (h d) -> p h d", h=4), [ps_r], ["VA"])
            for hl in range(4):
                attn_dense(4 * g + hl, hl, hl, 96, scale, QT, KT, lambda j: (0.0, []), ["KTrope"])
            flush()

    def load_group(L, g, nel):
        slot = g % 2
        load_cols(WB[:, slot * SLOT:slot * SLOT + nel], dr[f"wb{L}"][g], nel, f"WBs{slot}", f"WBs{slot}")

    def mixer_swa(L):
        QT = Q32[:, 0:8192].rearrange("p (h n) -> p h n", h=4)
        KT = Q32[:, 8192:16384].rearrange("p (h n) -> p h n", h=4)
        t.dma("sp", "sinks", lambda e: e.dma_start(out=sinks_bc, in_=dr[f"sinks{L}"].partition_broadcast(128)),
              writes=["sinks"])
        t.dma("sp", "X16", lambda e: [e.dma_start(out=dm4, in_=dr["dm4"][:, :]),
                                      e.dma_start(out=dm4f, in_=dr["dm4f"][:, :])], writes=X16_ALL, n=2)
        t.op("act", lambda e: e.activation(out=esink, in_=sinks_bc, func=AF.Exp), reads=["sinks"], writes=["esink"])
        for g in range(4):
            slot = g % 2
            if g + 1 < 4:
                load_group(L, g + 1, W_END)
            base = slot * SLOT
            Wq = WB[:, base + W_WQ:base + W_WK].rearrange("p (c n) -> p c n", c=8)
            Wk = WB[:, base + W_WK:base + W_WV].rearrange("p (c n) -> p c n", c=8)
            Wv = WB[:, base + W_WV:base + W_END].rearrange("p (c n) -> p c n", c=8)
            wres = f"WBs{slot}"
            pend = None

            def tpose(ti, qt_, qt_r, kt_, kt_r):
                tb, tb_r = tring.next()

                def trq(e, tb=tb, qt_=qt_, kt_=kt_):
                    for hl in range(4):
                        e.transpose(out=tb[0:64, hl * 128:(hl + 1) * 128], in_=qt_[:, hl * 64:(hl + 1) * 64],
                                    identity=ident[:, :])
                    return e.transpose(out=tb[0:64, 512:640], in_=kt_[:, 0:64], identity=ident[:, :])
                t.op("pe", trq, reads=[qt_r, kt_r, "ident"], writes=[tb_r])
                t.op("dve", lambda e, tb=tb, ti=ti: e.tensor_copy(
                    out=QT[0:64, :, ti * 128:(ti + 1) * 128],
                    in_=tb[0:64, 0:512].rearrange("p (h n) -> p h n", h=4)),
                    reads=[tb_r], writes=[f"QT{i}" for i in range(4)])
                t.op("dve", lambda e, tb=tb, ti=ti: e.tensor_copy(
                    out=KT[0:64, 0, ti * 128:(ti + 1) * 128], in_=tb[0:64, 512:640]),
                    reads=[tb_r], writes=["KT0"])

            for ti in range(NT):
                psq, psq_r = proj6.next()
                psk, psk_r = proj6.next()
                psv, psv_r = proj6.next()

                def pmm(e, psq=psq, psk=psk, psv=psv, ti=ti, Wq=Wq, Wk=Wk, Wv=Wv):
                    ins = None
                    for c in range(8):
                        e.matmul(psq[:, 0:256], lhsT=xT[:, c, ti * 128:(ti + 1) * 128], rhs=Wq[:, c, :],
                                 start=(c == 0), stop=(c == 7))
                    for c in range(8):
                        e.matmul(psk[:, 0:64], lhsT=xT[:, c, ti * 128:(ti + 1) * 128], rhs=Wk[:, c, :],
                                 start=(c == 0), stop=(c == 7))
                    for c in range(8):
                        ins = e.matmul(psv[:, 0:64], lhsT=xT[:, c, ti * 128:(ti + 1) * 128], rhs=Wv[:, c, :],
                                       start=(c == 0), stop=(c == 7))
                    return ins
                t.op("pe", pmm, reads=[xT_res[ti], wres], writes=[psq_r, psk_r, psv_r])
                qt_, qt_r = qtring.next()
                kt_, kt_r = ktring.next()
                t.op("dve", lambda e, psq=psq, qt_=qt_: e.tensor_copy(out=qt_[:, 0:256], in_=psq[:, 0:256]),
                     reads=[psq_r], writes=[qt_r])
                t.op("dve", lambda e, psk=psk, kt_=kt_: e.tensor_copy(out=kt_[:, 0:64], in_=psk[:, 0:64]),
                     reads=[psk_r], writes=[kt_r])
                copy_op("dve", VA[:, ti, 0, 0:64], psv[:, 0:64], [psv_r], ["VA"])
                if pend is not None:
                    tpose(*pend)
                pend = (ti, qt_, qt_r, kt_, kt_r)
            tpose(*pend)
            for hl in range(4):
                attn_band(4 * g + hl, hl, QT, KT)
            flush()

    def mixer_fox(L):
        QT = Q32[:, 0:8192].rearrange("p (h n) -> p h n", h=4)
        KT = Q32[:, 8192:16384].rearrange("p (h n) -> p h n", h=4)
        Zg = X16[:, 0:2080].bitcast(BF16).rearrange("p (t h c) -> p t h c", t=NT, h=4)
        nlf = X16[:, 2080:2336].rearrange("p (t h) -> p t h", t=NT)
        negck = X16[:, 2336:2592].rearrange("p (t h) -> p t h", t=NT)
        cum8 = X16[:, 2592:2720].bitcast(BF16).rearrange("p (t h) -> p t h", t=NT)
        Wf = X16[:, 2720:2784].bitcast(BF16).rearrange("p (c n) -> p c n", c=8)
        Zg = None
        t.dma("pool", "wf", lambda e: e.dma_start(out=X16[:, 2720:2784].bitcast(BF16), in_=dr[f"wf{L}"][:, :]),
              writes=X16_ALL)
        t.dma("sp", "bf", lambda e: e.dma_start(out=bf_bc, in_=dr[f"bf{L}"].partition_broadcast(128)),
              writes=["bf"])
        t.op("pool", lambda e: e.memset(X16[:, 0:2080], 0.0), writes=X16_ALL)
        t.op("pool", lambda e: e.memset(carry[0], 0.0), writes=["carry0"])
        t.op("pool", lambda e: e.memset(KT[64:65, :, :], 1.0), writes=["KTrope"])
        for ti in range(NT):
            ps, ps_r = short.next()

            def fmm(e, ps=ps, ti=ti):
                for c in range(8):
                    ins = e.matmul(ps[:, 0:16], lhsT=xT[:, c, ti * 128:(ti + 1) * 128], rhs=Wf[:, c, :],
                                   start=(c == 0), stop=(c == 7))
                return ins
            t.op("pe", fmm, reads=[xT_res[ti], "Wf"], writes=[ps_r])
            t.op("dve", lambda e, ps=ps: e.tensor_tensor(out=zt, in0=ps[:, 0:16], in1=bf_bc, op=ALU.add),
                 reads=[ps_r, "bf"], writes=["zt"])
            t.op("act", lambda e: e.activation(out=et, in_=zt, func=AF.Exp, scale=-1.0), reads=["zt"], writes=["et"])
            t.op("act", lambda e, ti=ti: e.activation(out=nlf[:, ti, :], in_=et, func=AF.Ln, bias=1.0, scale=1.0),
                 reads=["et"], writes=[f"nlf{ti}"])
            ps2, ps2_r = short.next()

            def cmm(e, ps2=ps2, ti=ti):
                e.matmul(ps2[:, 0:16], lhsT=u32[:, :], rhs=nlf[:, ti, :], start=True, stop=False,
                         skip_group_check=True)
                return e.matmul(ps2[:, 16:32], lhsT=ones32[:, :], rhs=nlf[:, ti, :], start=False, stop=True,
                                skip_group_check=True)
            t.op("pe", cmm, reads=[f"nlf{ti}", "u32", "ones32"], writes=[ps2_r])
            ca, cb = carry[ti % 2], carry[(ti + 1) % 2]
            car, cbr = f"carry{ti % 2}", f"carry{(ti + 1) % 2}"
            t.op("dve", lambda e, ps2=ps2, ti=ti, ca=ca: e.tensor_tensor(out=negck[:, ti, :], in0=ps2[:, 0:16],
                                                                          in1=ca, op=ALU.add),
                 reads=[ps2_r, car], writes=[f"negck{ti}"])
            t.op("dve", lambda e, ps2=ps2, ca=ca, cb=cb: e.tensor_tensor(out=cb, in0=ps2[:, 16:32], in1=ca,
                                                                          op=ALU.add),
                 reads=[ps2_r, car], writes=[cbr])
            t.op("pool", lambda e, ti=ti: e.tensor_scalar(out=cum8[:, ti, :], in0=negck[:, ti, :], scalar1=-8.0,
                                                          scalar2=None, op0=ALU.mult),
                 reads=[f"negck{ti}"], writes=["cum8"])
        nk_all = [f"negck{ti}" for ti in range(NT)]
        for g in range(4):
            slot = g % 2
            if g + 1 < 4:
                load_group(L, g + 1, F_END)
            base = slot * SLOT
            Wq = WB[:, base + F_WQ:base + F_WK].rearrange("p (c n) -> p c n", c=8)
            Wk = WB[:, base + F_WK:base + F_WV].rearrange("p (c n) -> p c n", c=8)
            Wv = WB[:, base + F_WV:base + F_END].rearrange("p (c n) -> p c n", c=8)
            wres = f"WBs{slot}"
            pend = None

            def tpose(ti, qt_, qt_r, kt_, kt_r):
                tb, tb_r = tring.next()

                def trq(e, tb=tb, qt_=qt_, kt_=kt_):
                    ins = None
                    for hl in range(4):
                        e.transpose(out=tb[0:65, hl * 128:(hl + 1) * 128], in_=qt_[:, hl * 65:(hl + 1) * 65],
                                    identity=ident[:, :])
                        ins = e.transpose(out=tb[0:64, 512 + hl * 128:512 + (hl + 1) * 128],
                                          in_=kt_[:, hl * 64:(hl + 1) * 64], identity=ident[:, :])
                    return ins
                t.op("pe", trq, reads=[qt_r, qt_r + "c", kt_r, "ident"], writes=[tb_r])
                t.op("dve", lambda e, tb=tb, ti=ti: e.tensor_copy(
                    out=QT[0:65, :, ti * 128:(ti + 1) * 128],
                    in_=tb[0:65, 0:512].rearrange("p (h n) -> p h n", h=4)),
                    reads=[tb_r], writes=[f"QT{i}" for i in range(4)])
                t.op("dve", lambda e, tb=tb, ti=ti: e.tensor_copy(
                    out=KT[0:64, :, ti * 128:(ti + 1) * 128],
                    in_=tb[0:64, 512:1024].rearrange("p (h n) -> p h n", h=4)),
                    reads=[tb_r], writes=[f"KT{i}" for i in range(4)])

            for ti in range(NT):
                psq, psq_r = proj6.next()
                psk, psk_r = proj6.next()
                psv, psv_r = proj6.next()

                def pmm(e, psq=psq, psk=psk, psv=psv, ti=ti, Wq=Wq, Wk=Wk, Wv=Wv):
                    ins = None
                    for ps, W in ((psq, Wq), (psk, Wk), (psv, Wv)):
                        for c in range(8):
                            ins = e.matmul(ps[:, 0:256], lhsT=xT[:, c, ti * 128:(ti + 1) * 128], rhs=W[:, c, :],
                                           start=(c == 0), stop=(c == 7))
                    return ins
                t.op("pe", pmm, reads=[xT_res[ti], wres], writes=[psq_r, psk_r, psv_r])
                qt_, qt_r = qtring.next()
                kt_, kt_r = ktring.next()
                t.op("dve", lambda e, psq=psq, qt_=qt_: e.tensor_copy(
                    out=qt_[:, :].rearrange("p (h c) -> p h c", h=4)[:, :, 0:64],
                    in_=psq[:, 0:256].rearrange("p (h d) -> p h d", h=4)), reads=[psq_r], writes=[qt_r])
                t.op("pool", lambda e, qt_=qt_, ti=ti, g=g: e.tensor_copy(
                    out=qt_[:, :].rearrange("p (h c) -> p h c", h=4)[:, :, 64], in_=cum8[:, ti, 4 * g:4 * g + 4]),
                    reads=["cum8"], writes=[qt_r + "c"])
                t.op("act", lambda e, psk=psk, kt_=kt_: e.activation(out=kt_[:, :], in_=psk[:, 0:256], func=AF.Copy),
                     reads=[psk_r], writes=[kt_r])
                copy_op("dve", VA[:, ti, :, 0:64], psv[:, 0:256].rearrange("p (h d) -> p h d", h=4), [psv_r], ["VA"])
                if pend is not None:
                    tpose(*pend)
                pend = (ti, qt_, qt_r, kt_, kt_r)
            tpose(*pend)
            for hl in range(4):
                h = 4 * g + hl
                attn_dense(h, hl, hl, 65, 0.125, QT, KT,
                           lambda j, h=h: (negck[:, j, h:h + 1], [f"negck{j}"]), ["KTrope"])
            flush()

    def phase_C(L, src, src_res, dst, dst_res):
        Wo = WX[:, :].rearrange("p (c n) -> p c n", c=8)
        lng = X16[:, 0:1024]
        lnb = X16[:, 1024:2048]
        RC = R32[:, :].bitcast(F32)
        sqjunk = Q32[:, 0:1024]
        xcring = Ring([(RC[:, i * 1024:(i + 1) * 1024], f"xc{i}") for i in range(4)])
        zring = Ring([(RC[:, 4096 + i * 1024:4096 + (i + 1) * 1024], f"zb{i}") for i in range(2)])
        xoring = Ring([(RC[:, 6144 + i * 1024:6144 + (i + 1) * 1024], f"xoR{i}") for i in range(2)])
        sets = [dict(stats=small[:, 144:156], mv=small[:, 156:158], lnr=small[:, 6:8], nmr=small[:, 158:159]),
                dict(stats=small[:, 164:176], mv=small[:, 176:178], lnr=small[:, 178:180], nmr=small[:, 180:181])]
        t.dma("sp", "lngb", lambda e: [e.dma_start(out=lng, in_=dr["lng"][L].partition_broadcast(128)),
                                       e.dma_start(out=lnb, in_=dr["lnb"][L].partition_broadcast(128))],
              writes=X16_ALL, n=2)
        def front(ti):
            xs, xs_r = xcring.next()
            t.dma("sp", xs_r, lambda e, xs=xs, ti=ti: e.dma_start(out=xs, in_=src[ti * 128:(ti + 1) * 128, :]),
                  reads=[f"{src_res}{ti}"], writes=[xs_r])
            tb, tb_r = tring.next()

            def tr8(e, tb=tb, ti=ti):
                for k in range(8):
                    ins = e.transpose(out=tb[:, k * 128:(k + 1) * 128], in_=GB[:, ti, k * 128:(k + 1) * 128],
                                      identity=ident[:, :])
                return ins
            t.op("pe", tr8, reads=[f"G{ti}_{k}" for k in range(8)] + ["ident"], writes=[tb_r])
            og, og_r = ogring.next()
            t.op("act", lambda e, tb=tb, og=og: e.activation(out=og[:, :], in_=tb[:, :], func=AF.Copy),
                 reads=[tb_r], writes=[og_r])
            return xs, xs_r, og, og_r

        def rest(ti, xs, xs_r, og, og_r):
            ogv = og[:, :].rearrange("p (c n) -> p c n", c=8)
            pss = []
            for half in range(2):
                ps, ps_r = short.next()
                pss.append((ps, ps_r))

                def ymm(e, ps=ps, ogv=ogv, half=half):
                    for k in range(8):
                        ins = e.matmul(ps[:, :], lhsT=ogv[:, k, :], rhs=Wo[:, k, half * 512:(half + 1) * 512],
                                       start=(k == 0), stop=(k == 7))
                    return ins
                t.op("pe", ymm, reads=[og_r, "WX"], writes=[ps_r])
            zb, zb_r = zring.next()
            st = sets[ti % 2]
            sn = f"s{ti % 2}"
            stats_, mv_, lnr_, nmr_ = st["stats"], st["mv"], st["lnr"], st["nmr"]
            for half in range(2):
                ps, ps_r = pss[half]
                hs = slice(half * 512, (half + 1) * 512)
                t.op("dve", lambda e, ps=ps, xs=xs, hs=hs, zb=zb, half=half, stats_=stats_: e.scalar_tensor_tensor(
                    out=zb[:, hs], in0=xs[:, hs], scalar=ALPHA, in1=ps[:, :], op0=ALU.mult, op1=ALU.add,
                    accum_out=stats_[:, half:half + 1]),
                    reads=[ps_r, xs_r], writes=[f"{zb_r}h{half}", f"st{half}{sn}"])
            t.op("act", lambda e, zb=zb, stats_=stats_: e.activation(out=sqjunk, in_=zb, func=AF.Square,
                                                                      accum_out=stats_[:, 2:3]),
                 reads=[f"{zb_r}h0", f"{zb_r}h1"], writes=["sqjunk", f"st2{sn}"])
            t.op("dve", lambda e, mv_=mv_, stats_=stats_: e.tensor_scalar(
                out=mv_[:, 0:1], in0=stats_[:, 0:1], scalar1=stats_[:, 1:2], scalar2=1.0 / D,
                op0=ALU.add, op1=ALU.mult), reads=[f"st0{sn}", f"st1{sn}"], writes=[f"mean{sn}"])
            t.op("dve", lambda e, mv_=mv_, stats_=stats_: e.tensor_tensor(
                out=stats_[:, 3:4], in0=mv_[:, 0:1], in1=mv_[:, 0:1], op=ALU.mult),
                reads=[f"mean{sn}"], writes=[f"m2{sn}"])
            t.op("dve", lambda e, mv_=mv_, stats_=stats_: e.scalar_tensor_tensor(
                out=mv_[:, 1:2], in0=stats_[:, 2:3], scalar=1.0 / D, in1=stats_[:, 3:4],
                op0=ALU.mult, op1=ALU.subtract), reads=[f"st2{sn}", f"m2{sn}"], writes=[f"mv{sn}"])
            t.op("act", lambda e, mv_=mv_, lnr_=lnr_: e.activation(out=lnr_[:, 0:1], in_=mv_[:, 1:2], func=AF.Ln,
                                                                   bias=LN_EPS, scale=1.0),
                 reads=[f"mv{sn}"], writes=[f"lnr0{sn}"])
            t.op("act", lambda e, lnr_=lnr_: e.activation(out=lnr_[:, 1:2], in_=lnr_[:, 0:1], func=AF.Exp, scale=-0.5),
                 reads=[f"lnr0{sn}"], writes=[f"lnr1{sn}"])
            t.op("dve", lambda e, mv_=mv_, lnr_=lnr_, nmr_=nmr_: e.scalar_tensor_tensor(
                out=nmr_, in0=mv_[:, 0:1], scalar=-1.0, in1=lnr_[:, 1:2], op0=ALU.mult, op1=ALU.mult),
                reads=[f"mean{sn}", f"lnr1{sn}"], writes=[f"nmr{sn}"])
            t.op("act", lambda e, zb=zb, lnr_=lnr_, nmr_=nmr_: e.activation(
                out=zb, in_=zb, func=AF.Identity, bias=nmr_, scale=lnr_[:, 1:2]),
                reads=[f"{zb_r}h0", f"{zb_r}h1", f"lnr1{sn}", f"nmr{sn}"], writes=[f"{zb_r}h0", f"{zb_r}h1"])
            xo, xo_r = xoring.next()
            for half, eng in ((0, "dve"), (1, "pool")):
                hs = slice(half * 512, (half + 1) * 512)
                t.op(eng, lambda e, zb=zb, xo=xo, hs=hs: e.tensor_tensor(out=xo[:, hs], in0=zb[:, hs], in1=lng[:, hs],
                                                                         op=ALU.mult),
                     reads=[f"{zb_r}h{half}", "lng"], writes=[f"{xo_r}h{half}"])
                t.op(eng, lambda e, xo=xo, hs=hs: e.tensor_tensor(out=xo[:, hs], in0=xo[:, hs], in1=lnb[:, hs],
                                                                  op=ALU.add),
                     reads=[f"{xo_r}h{half}", "lnb"], writes=[f"{xo_r}h{half}"])
            t.dma("sp", xo_r, lambda e, ti=ti, xo=xo: e.dma_start(out=dst[ti * 128:(ti + 1) * 128, :], in_=xo),
                  reads=[f"{xo_r}h0", f"{xo_r}h1"], writes=[f"{dst_res}{ti}"])

        ctxs = {0: front(0), 1: front(1)}
        for ti in range(NT):
            if ti + 2 < NT:
                ctxs[ti + 2] = front(ti + 2)
            rest(ti, *ctxs.pop(ti))

    def load_b_weights(L):
        k = KINDS[L]
        if k == 0:
            load_cols(WB, dr[f"wb{L}"], M_END, "WBm", "WB")
        else:
            load_group(L, 0, W_END if k == 1 else F_END)

    STOP = os.environ.get("KSTOP", "")
    load_consts()
    for L in range(nlayers):
        if STOP == "consts":
            break
        kind = KINDS[L]
        src, src_res = (x_in, "xin") if L == 0 else (scratch[(L - 1) % 2], f"xs{(L - 1) % 2}_")
        dst, dst_res = (y_out, "yout") if L == nlayers - 1 else (scratch[L % 2], f"xs{L % 2}_")
        pre = L > 0 and kind != 0
        if pre:
            load_wx(f"wo{L}")
        else:
            load_wx(f"wg{L}")
        if L == 0:
            load_b_weights(L)
        if kind == 0:
            t.dma("sp", "gq", lambda e, L=L: e.dma_start(out=gq[:, :], in_=dr[f"gq{L}"].partition_broadcast(128)),
                  writes=["gq"])
            t.dma("sp", "X16", lambda e: [e.dma_start(out=X16[64:96, 0:2048], in_=dr["cosf"][:, :]),
                                          e.dma_start(out=X16[64:96, 2048:4096], in_=dr["sinf"][:, :])],
                  writes=X16_ALL, n=2)
        if STOP == "loads":
            break
        phase_A(L, src, src_res, kind, wg_in_q32=pre)
        if STOP == "A":
            break
        if pre:
            t.barrier()
        else:
            load_wx(f"wo{L}")
        if kind == 0:
            mixer_mla(L)
        elif kind == 1:
            mixer_swa(L)
        else:
            mixer_fox(L)
        if STOP == "B":
            break
        t.barrier()
        if L + 1 < nlayers:
            load_b_weights(L + 1)
            if KINDS[L + 1] != 0:
                load_cols(Q32[:, 1024:9216], dr[f"wg{L + 1}"], 8192, "WGQ", "WGQ")
        phase_C(L, src, src_res, dst, dst_res)
        t.barrier()
    t.wait_all("sp")
    with nc.Block() as block:
        t.emit(block)
    return nc


_CACHE = {}


def _run(inputs, nlayers=DEPTH):
    inp = {k: np.asarray(v) for k, v in inputs.items()}
    consts = _host_consts()
    weights = _host_weights(inp)
    shared = {}
    shared.update(consts)
    shared.update(weights)
    shapes = {k: v.shape for k, v in shared.items()}
    if nlayers not in _CACHE:
        _CACHE[nlayers] = build_program(nlayers, shapes)
    nc = _CACHE[nlayers]
    x = np.ascontiguousarray(inp["x"].astype(np.float32))
    import os
    ncores = int(os.environ.get("KCORES", "8"))
    in_maps = []
    for b in range(ncores):
        m = {"x": x[b]}
        m.update(shared)
        in_maps.append(m)
    res = run_bass_kernel_spmd(nc, in_maps, core_ids=list(range(ncores)))
    return np.stack([np.asarray(r["y"]) for r in res.results], axis=0).astype(np.float32)


def kernel(**inputs):
    return _run(inputs, DEPTH)
```
